# Optimizing a Trainium2 kernel written in Bass

```python
import jax, jax.numpy as jnp
from jax import lax
import numpy as np

D_MODEL = 2048
BATCH = 2
SEQ = 8192
DEPTH = 4

CHUNK = 64
HEAD_DIM = 128
D_MIX = D_MODEL
W_A = 6 * HEAD_DIM
W_B = 6 * HEAD_DIM
N_POOL_GROUPS = 4
POOL_WINDOWS = (2, 4, 8, 16)
G_C = HEAD_DIM
W_C = N_POOL_GROUPS * G_C
D_IN_TOT = 3 * W_A + 2 * W_B + W_C
K_SHORT = 3
K_CONFORMER = 31
K_FFN = 3
D_FF = 5632
RMS_EPS = 1e-6
LN_EPS = 1e-5

kernel_name = "hybrid_conv_pool_streaming_trunk"


def rms_norm(x, g):
    xf = x.astype(jnp.float32)
    y = xf * lax.rsqrt(jnp.mean(xf * xf, axis=-1, keepdims=True) + RMS_EPS)
    return (y * g.astype(jnp.float32)).astype(x.dtype)


def layer_norm(x, g, b):
    xf = x.astype(jnp.float32)
    mu = jnp.mean(xf, axis=-1, keepdims=True)
    var = jnp.mean(jnp.square(xf - mu), axis=-1, keepdims=True)
    y = (xf - mu) * lax.rsqrt(var + LN_EPS)
    return (y * g.astype(jnp.float32) + b.astype(jnp.float32)).astype(x.dtype)


def causal_depthwise_conv(x, w):
    k, c = w.shape
    return lax.conv_general_dilated(
        x, w[:, None, :].astype(x.dtype),
        window_strides=(1,), padding=[(k - 1, 0)],
        dimension_numbers=("NWC", "WIO", "NWC"),
        feature_group_count=c)


def trailing_mean_minus_self(x, window):
    s = x.shape[1]
    xf = x.astype(jnp.float32)
    c = jnp.cumsum(xf, axis=1)
    c_shift = jnp.pad(c, ((0, 0), (window, 0), (0, 0)))[:, :s]
    counts = jnp.minimum(jnp.arange(1, s + 1, dtype=jnp.float32), float(window))[None, :, None]
    return ((c - c_shift) / counts - xf).astype(x.dtype)


def short_gated_conv_mixer(u_a, conv_w):
    b_gate, c_gate, v = jnp.split(u_a, 3, axis=-1)
    return b_gate * causal_depthwise_conv(c_gate * v, conv_w)


def conformer_conv_mixer(u_b, conv_w, conv_b, ln_g, ln_b):
    val, gate = jnp.split(u_b, 2, axis=-1)
    glu = val * jax.nn.sigmoid(gate)
    y = causal_depthwise_conv(glu, conv_w) + conv_b
    y = layer_norm(y, ln_g, ln_b)
    return jax.nn.silu(y)


def multiscale_pool_mixer(u_c, pool_w, pool_scale):
    bsz, s, _ = u_c.shape
    groups = u_c.reshape(bsz, s, N_POOL_GROUPS, G_C)
    pooled = jnp.stack(
        [trailing_mean_minus_self(groups[:, :, gi], w) for gi, w in enumerate(POOL_WINDOWS)],
        axis=2)
    mixed = jnp.einsum("bsgc,gcd->bsgd", pooled, pool_w)
    return mixed.reshape(bsz, s, W_C) * pool_scale


def conv_gated_mlp(h, w_up, conv_w, conv_b, w_down):
    up = h @ w_up
    up = causal_depthwise_conv(up, conv_w) + conv_b
    gate, val = jnp.split(up, 2, axis=-1)
    return (jax.nn.silu(gate) * val) @ w_down


def setup_inputs(seed: int = 0) -> dict:
    key = jax.random.key(seed)
    ks = jax.random.split(key, 20)
    f32 = jnp.float32
    nrm = lambda k, shape, scale: jax.random.normal(k, shape, f32) * scale
    gain = lambda k, shape: 1.0 + 0.05 * jax.random.normal(k, shape, f32)
    return {
        "x": jax.random.normal(ks[0], (BATCH, SEQ, D_MODEL), f32),
        "norm_mix_pre": gain(ks[1], (DEPTH, D_MODEL)),
        "norm_mix_post": gain(ks[2], (DEPTH, D_MODEL)),
        "norm_ffn_pre": gain(ks[3], (DEPTH, D_MODEL)),
        "norm_ffn_post": gain(ks[4], (DEPTH, D_MODEL)),
        "w_in": nrm(ks[5], (DEPTH, D_MODEL, D_IN_TOT), D_MODEL ** -0.5),
        "conv_a_w": nrm(ks[6], (DEPTH, K_SHORT, W_A), K_SHORT ** -0.5),
        "conv_b_w": nrm(ks[7], (DEPTH, K_CONFORMER, W_B), K_CONFORMER ** -0.5),
        "conv_b_bias": nrm(ks[8], (DEPTH, W_B), 0.01),
        "ln_b_gain": gain(ks[9], (DEPTH, W_B)),
        "ln_b_bias": nrm(ks[10], (DEPTH, W_B), 0.01),
        "pool_w": nrm(ks[11], (DEPTH, N_POOL_GROUPS, G_C, G_C), G_C ** -0.5),
        "pool_scale": 1.0 + 0.1 * jax.random.normal(ks[12], (DEPTH, W_C), f32),
        "w_out": nrm(ks[13], (DEPTH, D_MIX, D_MODEL), D_MIX ** -0.5),
        "w_up": nrm(ks[14], (DEPTH, D_MODEL, 2 * D_FF), D_MODEL ** -0.5),
        "conv_ffn_w": nrm(ks[15], (DEPTH, K_FFN, 2 * D_FF), K_FFN ** -0.5),
        "conv_ffn_bias": nrm(ks[16], (DEPTH, 2 * D_FF), 0.01),
        "w_down": nrm(ks[17], (DEPTH, D_FF, D_MODEL), D_FF ** -0.5),
    }


def reference(x, norm_mix_pre, norm_mix_post, norm_ffn_pre, norm_ffn_post,
              w_in, conv_a_w, conv_b_w, conv_b_bias, ln_b_gain, ln_b_bias,
              pool_w, pool_scale, w_out, w_up, conv_ffn_w, conv_ffn_bias, w_down):
    for l in range(DEPTH):
        h = rms_norm(x, norm_mix_pre[l])
        u = h @ w_in[l]
        u_a = u[..., :3 * W_A]
        u_b = u[..., 3 * W_A:3 * W_A + 2 * W_B]
        u_c = u[..., 3 * W_A + 2 * W_B:]
        y_a = short_gated_conv_mixer(u_a, conv_a_w[l])
        y_b = conformer_conv_mixer(u_b, conv_b_w[l], conv_b_bias[l], ln_b_gain[l], ln_b_bias[l])
        y_c = multiscale_pool_mixer(u_c, pool_w[l], pool_scale[l])
        y = jnp.concatenate([y_a, y_b, y_c], axis=-1) @ w_out[l]
        x = x + rms_norm(y, norm_mix_post[l])
        h = rms_norm(x, norm_ffn_pre[l])
        f = conv_gated_mlp(h, w_up[l], conv_ffn_w[l], conv_ffn_bias[l], w_down[l])
        x = x + rms_norm(f, norm_ffn_post[l])
    return x
```

```python
import contextlib
import numpy as np
import concourse.bass as bass
import concourse.mybir as mybir
from concourse.bass_utils import run_bass_kernel_spmd

F32 = mybir.dt.float32
BF16 = mybir.dt.bfloat16
AF = mybir.ActivationFunctionType
ALU = mybir.AluOpType

N_CORES = 8
D = 2048
NF = 16
NFF = 44
DEPTH = 4
SEQ = 8192
TOK_PER_CORE = 2048
T = 640
PADL = 32
CH = 512
HALO = 128
WSLOT = 2816
NSLOT = 6
NSCR = 4
NRS = 4

NPL = 642
OFF_GMP, OFF_GMO, OFF_GFP, OFF_GFO = 0, 16, 32, 48
OFF_CA, OFF_CB, OFF_CBB, OFF_LNG, OFF_LNB, OFF_PS, OFF_CF, OFF_CFB = 64, 82, 268, 274, 280, 286, 290, 554
POOL_W = (2, 4, 8, 16)


class Sem:
    def __init__(self, h):
        self.h = h
        self.cnt = 0


class Eng:
    def __init__(self, name, e, sem):
        self.name = name
        self.e = e
        self.sem = sem
        self.seen = {}


class Res:
    __slots__ = ("w", "r")

    def __init__(self):
        self.w = None
        self.r = {}


def tiles_of(r0):
    n = T - r0
    h = ((n + 1) // 2 + 1) // 2 * 2
    return [(r0, r0 + h), (r0 + h, T)]


def build(n_layers, n_chunks, n_ranks=N_CORES):
    nc = bass.Bass("TRN2", target_bir_lowering=False)
    NT = HALO + CH * n_chunks
    xT_d = nc.dram_tensor("xT", [128, NF, NT], F32, kind="ExternalInput").ap()
    yT_d = nc.dram_tensor("yT", [128, NF, CH * n_chunks], F32, kind="ExternalOutput").ap()
    win_d = nc.dram_tensor("win", [n_layers, 34, 128, 2048], F32, kind="ExternalInput").ap()
    wout_d = nc.dram_tensor("wout", [n_layers, 16, 128, 2048], F32, kind="ExternalInput").ap()
    wup_d = nc.dram_tensor("wup", [n_layers, 88, 128, 2048], F32, kind="ExternalInput").ap()
    wdn_d = nc.dram_tensor("wdn", [n_layers, 32, 128, WSLOT], F32, kind="ExternalInput").ap()
    pool_d = nc.dram_tensor("poolw", [128, n_layers * 4 * 128], F32, kind="ExternalInput").ap()
    pvec_d = nc.dram_tensor("pvec", [128, n_layers * NPL], F32, kind="ExternalInput").ap()
    cvec_d = nc.dram_tensor("cvec", [128, 65], F32, kind="ExternalInput").ap()

    with contextlib.ExitStack() as st:
        def sb(name, shape, dt):
            return st.enter_context(nc.sbuf_tensor(name, shape, dt))

        def mksem(name):
            return Sem(st.enter_context(nc.semaphore(name)))

        xT = sb("xT_sb", [128, NF * T], F32)
        oT = sb("oT_sb", [128, NF * T], F32)
        hT = oT.bitcast(BF16)
        big = sb("big_sb", [128, NFF * T], BF16)
        bigf = big.bitcast(F32)
        wring = [sb(f"wring{i}", [128, WSLOT], BF16) for i in range(NSLOT)]
        scr = [sb(f"scr{i}", [128, T], F32) for i in range(NSCR)]
        rsb = [sb(f"rsb{i}", [128, T], F32) for i in range(NRS)]
        scrb = [t_.bitcast(BF16) for t_ in scr]
        pvec = sb("pvec_sb", [128, n_layers * NPL], F32)
        poolw = sb("poolw_sb", [128, n_layers * 4 * 128], BF16)
        cvec = sb("cvec_sb", [128, 65], F32)
        ones = sb("ones_sb", [128, 128], F32)
        onesb = sb("onesb_sb", [128, 128], BF16)
        epsb = sb("eps_sb", [128, 2], F32)
        ps = st.enter_context(nc.psum_tensor("ps", [128, 8 * 512], F32))

        PE = Eng("pe", nc.tensor, mksem("s_pe"))
        ACT = Eng("act", nc.scalar, mksem("s_act"))
        DVE = Eng("dve", nc.vector, mksem("s_dve"))
        POOL = Eng("pool", nc.gpsimd, mksem("s_pool"))
        SP = Eng("sp", nc.sync, mksem("s_sp"))
        wsem = [mksem(f"s_w{i}") for i in range(NSLOT)]
        xsem = mksem("s_x")
        ysem = mksem("s_y")
        csem = [mksem(f"s_c{i}") for i in range(3)]

        block = st.enter_context(nc.Block())

        R_x = [Res() for _ in range(NF)]
        R_o = [Res() for _ in range(NF)]
        R_y = [Res() for _ in range(NF)]
        R_yb = [Res() for _ in range(6)]
        R_cv = [Res(), Res()]
        R_glu = [Res(), Res()]
        R_uc = Res()
        R_s = [Res(), Res()]
        R_act = [Res() for _ in range(NFF)]
        R_w = [Res() for _ in range(NSLOT)]
        R_scr = [Res() for _ in range(NSCR)]
        R_rs = [Res() for _ in range(NRS)]
        R_ps = [Res() for _ in range(8)]
        R_pv, R_pw, R_cvec, R_ones, R_eps = Res(), Res(), Res(), Res(), Res()

        def _deps(E, reads, writes, pe):
            deps = {}

            def add(s, v):
                if pe and s is E.sem:
                    return
                if deps.get(s, 0) < v:
                    deps[s] = v
            for r in reads:
                if r.w is not None:
                    add(*r.w)
            for w in writes:
                if w.w is not None:
                    add(*w.w)
                for s, v in w.r.items():
                    add(s, v)
            for s, v in deps.items():
                if E.seen.get(s, 0) < v:
                    E.e.wait_ge(s.h, v)
                    E.seen[s] = v

        def _commit(tok, reads, writes):
            s, v = tok
            for r in reads:
                if r.r.get(s, 0) < v:
                    r.r[s] = v
            for w in writes:
                w.w = tok
                w.r = {}

        def op(E, fn, reads=(), writes=(), pe=False):
            _deps(E, reads, writes, pe)
            ins = fn()
            E.sem.cnt += 1
            ins.then_inc(E.sem.h, 1)
            tok = (E.sem, E.sem.cnt)
            _commit(tok, reads, writes)
            return tok

        def dma(E, fn, sem, reads=(), writes=()):
            _deps(E, reads, writes, False)
            ins = fn()
            sem.cnt += 16
            ins.then_inc(sem.h, 16)
            tok = (sem, sem.cnt)
            _commit(tok, reads, writes)
            return tok

        state = {"bank": 7, "scr": 0, "rs": 0, "slot": 0, "pinned": set()}

        def alloc_bank():
            while True:
                state["bank"] = (state["bank"] + 1) % 8
                if state["bank"] not in state["pinned"]:
                    return state["bank"]

        def nscr():
            state["scr"] = (state["scr"] + 1) % NSCR
            return state["scr"]

        def nrs():
            state["rs"] = (state["rs"] + 1) % NRS
            return state["rs"]

        def PS(b, c0, c1):
            return ps[:, b * 512 + c0: b * 512 + c1]

        def X(ft, t0, t1):
            return xT[:, ft * T + t0: ft * T + t1]

        def H(ft, t0, t1):
            return hT[:, ft * T + t0: ft * T + t1]

        def O(m, t0, t1):
            return oT[:, m * T + t0: m * T + t1]

        def Y(k, t0, t1):
            return big[:, k * T + t0: k * T + t1]

        def A(j, t0, t1):
            return big[:, j * T + t0: j * T + t1]

        YB0 = 8 * T
        PB = PADL + T
        CV0 = 14 * T

        def YB(j, t0, t1):
            return bigf[:, YB0 + j * T + t0: YB0 + j * T + t1]

        def padbuf(idx):
            base = CV0 + idx * PB + PADL

            def f(t0, t1):
                return bigf[:, base + t0: base + t1]
            return f
        CV = [padbuf(0), padbuf(1)]
        GLU = [padbuf(2), padbuf(3)]
        UC = padbuf(4)
        SS = [padbuf(5), padbuf(6)]

        def pv(l, off, n=1):
            return pvec[:, l * NPL + off: l * NPL + off + n]

        act = nc.scalar.activation
        V = nc.vector

        def load_w(src_ap, nelem):
            s = state["slot"]
            state["slot"] = (s + 1) % NSLOT
            dma(POOL, lambda: nc.gpsimd.dma_start(out=wring[s][:, 0:nelem], in_=src_ap), wsem[s],
                writes=[R_w[s]])
            return s

        def mm_group(bank, n, pairs, reads):
            def fn():
                ins = None
                last = len(pairs) - 1
                for i, (l_, r_) in enumerate(pairs):
                    ins = nc.tensor.matmul(PS(bank, 0, n), l_, r_, start=(i == 0), stop=(i == last))
                return ins
            return op(PE, fn, reads=reads, writes=[R_ps[bank]], pe=True)

        def ones_mm(bank, n, rhs_ap, rhs_res, first, last):
            return op(PE, lambda: nc.tensor.matmul(PS(bank, 0, n), ones[:, :], rhs_ap, start=first, stop=last),
                      reads=[rhs_res, R_ones], writes=[R_ps[bank]], pe=True)

        def ones_mm_bf(bank, n, rhs_ap, rhs_res, first, last):
            return op(PE, lambda: nc.tensor.matmul(PS(bank, 0, n), onesb[:, :], rhs_ap, start=first, stop=last),
                      reads=[rhs_res, R_ones], writes=[R_ps[bank]], pe=True)

        def proj(slot, rhs_fn, rhs_res, w0, t1):
            bank = alloc_bank()
            pairs = [(wring[slot][:, kt * 128:(kt + 1) * 128], rhs_fn(kt, w0, t1)) for kt in range(16)]
            mm_group(bank, t1 - w0, pairs, [R_w[slot]] + rhs_res)
            return bank

        def finish_rstd(banks, tl, r0, dim, epscol, chunk, use_mask):
            rs = nrs()
            for ti, (t0, t1) in enumerate(tl):
                n = t1 - t0
                op(ACT, lambda: act(out=rsb[rs][:, t0:t1], in_=PS(banks[ti], 0, n), func=AF.Sqrt,
                                    bias=epsb[:, epscol:epscol + 1], scale=1.0 / dim),
                   reads=[R_ps[banks[ti]], R_eps], writes=[R_rs[rs]])
            op(DVE, lambda: V.reciprocal(out=rsb[rs][:, r0:T], in_=rsb[rs][:, r0:T]),
               reads=[R_rs[rs]], writes=[R_rs[rs]])
            if use_mask and chunk == 0 and r0 < HALO:
                op(DVE, lambda: V.tensor_scalar(out=rsb[rs][:, r0:HALO], in0=rsb[rs][:, r0:HALO],
                                                scalar1=cvec[:, 0:1], scalar2=None, op0=ALU.mult),
                   reads=[R_rs[rs], R_cvec], writes=[R_rs[rs]])
            for b in banks:
                state["pinned"].discard(b)
            return rs

        def pre_norm(l, r0, goff):
            tl = tiles_of(r0)
            banks = [alloc_bank(), alloc_bank()]
            state["pinned"].update(banks)
            for ft in range(NF):
                sc = nscr()
                op(ACT, lambda: act(out=scrb[sc][:, r0:T], in_=X(ft, r0, T), func=AF.Square),
                   reads=[R_x[ft]], writes=[R_scr[sc]])
                for ti, (t0, t1) in enumerate(tl):
                    ones_mm_bf(banks[ti], t1 - t0, scrb[sc][:, t0:t1], R_scr[sc], ft == 0, ft == NF - 1)
            rs = finish_rstd(banks, tl, r0, float(D), 0, None, False)
            for ft in range(NF):
                op(DVE, lambda: V.scalar_tensor_tensor(out=H(ft, r0, T), in0=X(ft, r0, T),
                                                       scalar=pv(l, goff + ft), in1=rsb[rs][:, r0:T],
                                                       op0=ALU.mult, op1=ALU.mult),
                   reads=[R_x[ft], R_rs[rs], R_pv], writes=[R_o[ft // 2]])

        def post_norm_residual(l, r0, goff, banks, tl, chunk):
            rs = finish_rstd(banks, tl, r0, float(D), 0, chunk, True)
            for m in range(NF):
                op(DVE, lambda: V.scalar_tensor_tensor(out=O(m, r0, T), in0=O(m, r0, T),
                                                       scalar=pv(l, goff + m), in1=rsb[rs][:, r0:T],
                                                       op0=ALU.mult, op1=ALU.mult),
                   reads=[R_o[m], R_rs[rs], R_pv], writes=[R_o[m]])
                op(DVE, lambda: V.tensor_tensor(out=X(m, r0, T), in0=X(m, r0, T), in1=O(m, r0, T), op=ALU.add),
                   reads=[R_x[m], R_o[m]], writes=[R_x[m]])

        def evac_with_sq(bank, n, m, t0, t1, ssbank, first, last):
            op(DVE, lambda: V.tensor_copy(out=O(m, t0, t1), in_=PS(bank, 0, n)),
               reads=[R_ps[bank]], writes=[R_o[m]])

        def o_stats(r0, banks, tl):
            for m in range(NF):
                sc = nscr()
                op(ACT, lambda: act(out=scrb[sc][:, r0:T], in_=O(m, r0, T), func=AF.Square),
                   reads=[R_o[m]], writes=[R_scr[sc]])
                for ti, (t0, t1) in enumerate(tl):
                    ones_mm_bf(banks[ti], t1 - t0, scrb[sc][:, t0:t1], R_scr[sc], m == 0, m == NF - 1)

        def layer(l, chunk):
            a = 32 * l
            b = 32 * l + 28
            tla = tiles_of(a)
            tlb = tiles_of(b)
            hres = [R_o[i] for i in range(8)]

            pre_norm(l, a, OFF_GMP)

            def b_proj(j):
                sv = load_w(win_d[l, 18 + j], 2048)
                sg = load_w(win_d[l, 24 + j], 2048)
                gi = j % 2
                for (t0, t1) in tla:
                    n = t1 - t0
                    bv = proj(sv, H, hres, t0, t1)
                    bg = proj(sg, H, hres, t0, t1)
                    sc = nscr()
                    op(ACT, lambda: act(out=scr[sc][:, 0:n], in_=PS(bg, 0, n), func=AF.Sigmoid),
                       reads=[R_ps[bg]], writes=[R_scr[sc]])
                    op(DVE, lambda: V.tensor_tensor(out=GLU[gi](t0, t1), in0=scr[sc][:, 0:n], in1=PS(bv, 0, n),
                                                    op=ALU.mult),
                       reads=[R_scr[sc], R_ps[bv]], writes=[R_glu[gi]])

            def b_conv(j):
                gi = j % 2
                op(ACT, lambda: act(out=YB(j, b, T), in_=GLU[gi](b - 30, T - 30), func=AF.Identity,
                                    bias=pv(l, OFF_CBB + j), scale=pv(l, OFF_CB + j * 31)),
                   reads=[R_glu[gi], R_pv], writes=[R_yb[j]])
                for k in range(1, 31):
                    op(DVE, lambda: V.scalar_tensor_tensor(out=YB(j, b, T), in0=GLU[gi](b - 30 + k, T - 30 + k),
                                                           scalar=pv(l, OFF_CB + j * 31 + k), in1=YB(j, b, T),
                                                           op0=ALU.mult, op1=ALU.add),
                       reads=[R_glu[gi], R_pv, R_yb[j]], writes=[R_yb[j]])

            def a_head(j):
                sb_ = load_w(win_d[l, j], 2048)
                sc_ = load_w(win_d[l, 6 + j], 2048)
                sv_ = load_w(win_d[l, 12 + j], 2048)
                ci = j % 2
                for (t0, t1) in tla:
                    n = t1 - t0
                    bb = proj(sb_, H, hres, t0, t1)
                    bc = proj(sc_, H, hres, t0, t1)
                    bv = proj(sv_, H, hres, t0, t1)
                    s1 = nscr()
                    op(ACT, lambda: act(out=scr[s1][:, 0:n], in_=PS(bc, 0, n), func=AF.Copy),
                       reads=[R_ps[bc]], writes=[R_scr[s1]])
                    op(DVE, lambda: V.tensor_tensor(out=CV[ci](t0, t1), in0=scr[s1][:, 0:n], in1=PS(bv, 0, n),
                                                    op=ALU.mult),
                       reads=[R_scr[s1], R_ps[bv]], writes=[R_cv[ci]])
                    s2 = nscr()
                    op(ACT, lambda: act(out=scr[s2][:, 0:n], in_=CV[ci](t0 - 2, t1 - 2), func=AF.Identity,
                                        scale=pv(l, OFF_CA + j * 3)),
                       reads=[R_cv[ci], R_pv], writes=[R_scr[s2]])
                    for k in (1, 2):
                        op(DVE, lambda: V.scalar_tensor_tensor(out=scr[s2][:, 0:n], in0=CV[ci](t0 - 2 + k, t1 - 2 + k),
                                                               scalar=pv(l, OFF_CA + j * 3 + k), in1=scr[s2][:, 0:n],
                                                               op0=ALU.mult, op1=ALU.add),
                           reads=[R_cv[ci], R_pv, R_scr[s2]], writes=[R_scr[s2]])
                    op(DVE, lambda: V.tensor_tensor(out=Y(j, t0, t1), in0=scr[s2][:, 0:n], in1=PS(bb, 0, n),
                                                    op=ALU.mult),
                       reads=[R_scr[s2], R_ps[bb]], writes=[R_y[j]])


            for j in range(6):
                b_proj(j)
                a_head(j)
                b_conv(j)

            for g in range(4):
                su = load_w(win_d[l, 30 + g], 2048)
                w = POOL_W[g]
                for (t0, t1) in tla:
                    n = t1 - t0
                    bu = proj(su, H, hres, t0, t1)
                    op(ACT, lambda: act(out=UC(t0, t1), in_=PS(bu, 0, n), func=AF.Copy),
                       reads=[R_ps[bu]], writes=[R_uc])
                src, src_res = UC, R_uc
                for k in range(1, g + 2):
                    sh = 1 << (k - 1)
                    lo = a - (w - (1 << k))
                    di = (k - 1) % 2
                    dst, dst_res = SS[di], R_s[di]
                    op(DVE, lambda: V.tensor_tensor(out=dst(lo, T), in0=src(lo, T), in1=src(lo - sh, T - sh),
                                                    op=ALU.add),
                       reads=[src_res], writes=[dst_res])
                    src, src_res = dst, dst_res
                op(DVE, lambda: V.scalar_tensor_tensor(out=Y(12 + g, a, T), in0=src(a, T), scalar=1.0 / w,
                                                       in1=UC(a, T), op0=ALU.mult, op1=ALU.subtract),
                   reads=[src_res, R_uc], writes=[R_y[12 + g]])
                if chunk == 0:
                    sc = nscr()
                    op(DVE, lambda: V.tensor_tensor(out=scr[sc][:, 0:16], in0=src(HALO, HALO + 16),
                                                    in1=cvec[:, 1 + g * 16: 17 + g * 16], op=ALU.mult),
                       reads=[src_res, R_cvec], writes=[R_scr[sc]])
                    op(DVE, lambda: V.tensor_tensor(out=Y(12 + g, HALO, HALO + 16), in0=scr[sc][:, 0:16],
                                                    in1=UC(HALO, HALO + 16), op=ALU.subtract),
                       reads=[R_scr[sc], R_uc], writes=[R_y[12 + g]])
                for (t0, t1) in tla:
                    n = t1 - t0
                    bm = alloc_bank()
                    pw_ap = poolw[:, (l * 4 + g) * 128:(l * 4 + g + 1) * 128]
                    op(PE, lambda: nc.tensor.matmul(PS(bm, 0, n), pw_ap, Y(12 + g, t0, t1), start=True, stop=True),
                       reads=[R_pw, R_y[12 + g]], writes=[R_ps[bm]], pe=True)
                    op(DVE, lambda: V.tensor_scalar(out=Y(12 + g, t0, t1), in0=PS(bm, 0, n),
                                                    scalar1=pv(l, OFF_PS + g), scalar2=None, op0=ALU.mult),
                       reads=[R_ps[bm], R_pv], writes=[R_y[12 + g]])

            b1 = [alloc_bank(), alloc_bank()]
            b2 = [alloc_bank(), alloc_bank()]
            state["pinned"].update(b1 + b2)
            for j in range(6):
                for ti, (t0, t1) in enumerate(tlb):
                    ones_mm(b1[ti], t1 - t0, YB(j, t0, t1), R_yb[j], j == 0, j == 5)
            for j in range(6):
                sc = nscr()
                op(ACT, lambda: act(out=scrb[sc][:, b:T], in_=YB(j, b, T), func=AF.Square),
                   reads=[R_yb[j]], writes=[R_scr[sc]])
                for ti, (t0, t1) in enumerate(tlb):
                    ones_mm_bf(b2[ti], t1 - t0, scrb[sc][:, t0:t1], R_scr[sc], j == 0, j == 5)
            rmean, rvar, rnmr = nrs(), nrs(), nrs()
            for ti, (t0, t1) in enumerate(tlb):
                n = t1 - t0
                op(DVE, lambda: V.tensor_scalar(out=rsb[rmean][:, t0:t1], in0=PS(b1[ti], 0, n),
                                                scalar1=1.0 / 768.0, scalar2=None, op0=ALU.mult),
                   reads=[R_ps[b1[ti]]], writes=[R_rs[rmean]])
            sc = nscr()
            op(DVE, lambda: V.tensor_tensor(out=scr[sc][:, b:T], in0=rsb[rmean][:, b:T], in1=rsb[rmean][:, b:T],
                                            op=ALU.mult),
               reads=[R_rs[rmean]], writes=[R_scr[sc]])
            for ti, (t0, t1) in enumerate(tlb):
                n = t1 - t0
                op(DVE, lambda: V.scalar_tensor_tensor(out=rsb[rvar][:, t0:t1], in0=PS(b2[ti], 0, n),
                                                       scalar=1.0 / 768.0, in1=scr[sc][:, t0:t1],
                                                       op0=ALU.mult, op1=ALU.subtract),
                   reads=[R_ps[b2[ti]], R_scr[sc]], writes=[R_rs[rvar]])
            for bb in b1 + b2:
                state["pinned"].discard(bb)
            op(ACT, lambda: act(out=rsb[rvar][:, b:T], in_=rsb[rvar][:, b:T], func=AF.Sqrt,
                                bias=epsb[:, 1:2], scale=1.0),
               reads=[R_rs[rvar], R_eps], writes=[R_rs[rvar]])
            op(DVE, lambda: V.reciprocal(out=rsb[rvar][:, b:T], in_=rsb[rvar][:, b:T]),
               reads=[R_rs[rvar]], writes=[R_rs[rvar]])
            op(DVE, lambda: V.scalar_tensor_tensor(out=rsb[rnmr][:, b:T], in0=rsb[rmean][:, b:T], scalar=-1.0,
                                                   in1=rsb[rvar][:, b:T], op0=ALU.mult, op1=ALU.mult),
               reads=[R_rs[rmean], R_rs[rvar]], writes=[R_rs[rnmr]])
            for j in range(6):
                sc = nscr()
                op(DVE, lambda: V.tensor_tensor(out=scr[sc][:, b:T], in0=YB(j, b, T), in1=rsb[rvar][:, b:T],
                                                op=ALU.mult),
                   reads=[R_yb[j], R_rs[rvar]], writes=[R_scr[sc]])
                op(DVE, lambda: V.tensor_tensor(out=scr[sc][:, b:T], in0=scr[sc][:, b:T], in1=rsb[rnmr][:, b:T],
                                                op=ALU.add),
                   reads=[R_scr[sc], R_rs[rnmr]], writes=[R_scr[sc]])
                op(ACT, lambda: act(out=Y(6 + j, b, T), in_=scr[sc][:, b:T], func=AF.Silu,
                                    bias=pv(l, OFF_LNB + j), scale=pv(l, OFF_LNG + j)),
                   reads=[R_scr[sc], R_pv], writes=[R_y[6 + j]])

            bss = [alloc_bank(), alloc_bank()]
            state["pinned"].update(bss)
            for m in range(NF):
                s = load_w(wout_d[l, m], 2048)
                for ti, (t0, t1) in enumerate(tlb):
                    bank = proj(s, Y, R_y, t0, t1)
                    evac_with_sq(bank, t1 - t0, m, t0, t1, bss[ti], m == 0, m == NF - 1)
            o_stats(b, bss, tlb)
            post_norm_residual(l, b, OFF_GMO, bss, tlb, chunk)

            pre_norm(l, b - 2, OFF_GFP)
            for j in range(NFF):
                sg = load_w(wup_d[l, j], 2048)
                sv = load_w(wup_d[l, NFF + j], 2048)
                for (t0, t1) in tlb:
                    n = t1 - t0
                    w0 = t0 - 2
                    bg = proj(sg, H, hres, w0, t1)
                    bv = proj(sv, H, hres, w0, t1)
                    accs = []
                    for (bank, m) in ((bg, j), (bv, NFF + j)):
                        sc = nscr()
                        op(DVE, lambda: V.tensor_scalar(out=scr[sc][:, 0:n], in0=PS(bank, 2, n + 2),
                                                        scalar1=pv(l, OFF_CF + m * 3 + 2), scalar2=pv(l, OFF_CFB + m),
                                                        op0=ALU.mult, op1=ALU.add),
                           reads=[R_ps[bank], R_pv], writes=[R_scr[sc]])
                        for k in (1, 0):
                            op(DVE, lambda: V.scalar_tensor_tensor(out=scr[sc][:, 0:n], in0=PS(bank, k, n + k),
                                                                   scalar=pv(l, OFF_CF + m * 3 + k),
                                                                   in1=scr[sc][:, 0:n], op0=ALU.mult, op1=ALU.add),
                               reads=[R_ps[bank], R_pv, R_scr[sc]], writes=[R_scr[sc]])
                        accs.append(sc)
                    sg_, sv_ = accs
                    op(ACT, lambda: act(out=scr[sg_][:, 0:n], in_=scr[sg_][:, 0:n], func=AF.Silu),
                       reads=[R_scr[sg_]], writes=[R_scr[sg_]])
                    op(DVE, lambda: V.tensor_tensor(out=A(j, t0, t1), in0=scr[sg_][:, 0:n], in1=scr[sv_][:, 0:n],
                                                    op=ALU.mult),
                       reads=[R_scr[sg_], R_scr[sv_]], writes=[R_act[j]])

            bss = [alloc_bank(), alloc_bank()]
            state["pinned"].update(bss)
            for m in range(NF):
                s0 = load_w(wdn_d[l, 2 * m], WSLOT)
                s1 = load_w(wdn_d[l, 2 * m + 1], WSLOT)
                for ti, (t0, t1) in enumerate(tlb):
                    bank = alloc_bank()
                    pairs = []
                    for kt in range(NFF):
                        ws = wring[s0] if kt < 22 else wring[s1]
                        kk = kt % 22
                        pairs.append((ws[:, kk * 128:(kk + 1) * 128], A(kt, t0, t1)))
                    mm_group(bank, t1 - t0, pairs, [R_w[s0], R_w[s1]] + R_act)
                    evac_with_sq(bank, t1 - t0, m, t0, t1, bss[ti], m == 0, m == NF - 1)
            o_stats(b, bss, tlb)
            post_norm_residual(l, b, OFF_GFO, bss, tlb, chunk)

        dma(SP, lambda: nc.sync.dma_start(out=pvec[:, :], in_=pvec_d), csem[0], writes=[R_pv])
        dma(SP, lambda: nc.sync.dma_start(out=cvec[:, :], in_=cvec_d), csem[1], writes=[R_cvec])
        dma(POOL, lambda: nc.gpsimd.dma_start(out=poolw[:, :], in_=pool_d), csem[2], writes=[R_pw])
        op(DVE, lambda: V.memset(ones[:, :], 1.0), writes=[R_ones])
        op(DVE, lambda: V.memset(onesb[:, :], 1.0), writes=[R_ones])
        op(DVE, lambda: V.memset(epsb[:, 0:1], 1e-6), writes=[R_eps])
        op(DVE, lambda: V.memset(epsb[:, 1:2], 1e-5), writes=[R_eps])
        op(DVE, lambda: V.memset(oT[:, :], 0.0), writes=R_o)
        op(DVE, lambda: V.memset(bigf[:, :], 0.0),
           writes=R_y + R_yb + R_cv + R_glu + [R_uc] + R_s + R_act)
        for i in range(NSCR):
            op(DVE, lambda: V.memset(scr[i][:, :], 0.0), writes=[R_scr[i]])
        for i in range(NRS):
            op(DVE, lambda: V.memset(rsb[i][:, :], 0.0), writes=[R_rs[i]])

        xT3 = xT[:, :].rearrange("p (f t) -> p f t", t=T)
        for c in range(n_chunks):
            dma(SP, lambda: nc.sync.dma_start(out=xT3, in_=xT_d[:, :, c * CH: c * CH + T]), xsem, writes=R_x)
            for l in range(n_layers):
                layer(l, c)
            dma(SP, lambda: nc.sync.dma_start(out=yT_d[:, :, c * CH:(c + 1) * CH], in_=xT3[:, :, HALO:T]),
                ysem, reads=R_x)
        nc.sync.wait_ge(ysem.h, ysem.cnt)
    return nc


def _fm(vec):
    return np.ascontiguousarray(vec.reshape(-1, 128).T)


def _pack_layer_params(p, l):
    cols = [
        _fm(p["norm_mix_pre"][l]), _fm(p["norm_mix_post"][l]),
        _fm(p["norm_ffn_pre"][l]), _fm(p["norm_ffn_post"][l]),
        p["conv_a_w"][l].reshape(3, 6, 128).transpose(2, 1, 0).reshape(128, 18),
        p["conv_b_w"][l].reshape(31, 6, 128).transpose(2, 1, 0).reshape(128, 186),
        _fm(p["conv_b_bias"][l]), _fm(p["ln_b_gain"][l]), _fm(p["ln_b_bias"][l]),
        _fm(p["pool_scale"][l]),
        p["conv_ffn_w"][l].reshape(3, 88, 128).transpose(2, 1, 0).reshape(128, 264),
        _fm(p["conv_ffn_bias"][l]),
    ]
    out = np.concatenate(cols, axis=1).astype(np.float32)
    assert out.shape == (128, NPL)
    return out


def _tile_w(w, n_m):
    return np.ascontiguousarray(w.reshape(16, 128, n_m, 128).transpose(2, 1, 0, 3).reshape(n_m, 128, 2048))


def _tile_wdn(w):
    return np.ascontiguousarray(
        w.reshape(2, 22, 128, 16, 128).transpose(3, 0, 2, 1, 4).reshape(32, 128, WSLOT))


def _layer_weights(p, layers):
    win = np.stack([_tile_w(p["w_in"][l], 34) for l in layers])
    wout = np.stack([_tile_w(p["w_out"][l], 16) for l in layers])
    wup = np.stack([_tile_w(p["w_up"][l], 88) for l in layers])
    wdn = np.stack([_tile_wdn(p["w_down"][l]) for l in layers])
    poolw = np.concatenate([p["pool_w"][l].transpose(1, 0, 2).reshape(128, 4 * 128) for l in layers], axis=1)
    pvec = np.concatenate([_pack_layer_params(p, l) for l in layers], axis=1)
    return dict(win=win, wout=wout, wup=wup, wdn=wdn,
                poolw=np.ascontiguousarray(poolw.astype(np.float32)),
                pvec=np.ascontiguousarray(pvec))


def _core_consts(core):
    cv = np.zeros((128, 65), np.float32)
    start = (core * TOK_PER_CORE) % SEQ == 0
    cv[:, 0] = 0.0 if start else 1.0
    for g, w in enumerate(POOL_W):
        for i in range(16):
            cv[:, 1 + g * 16 + i] = 1.0 / (min(i + 1, w) if start else w)
    return cv


def _shard_x(xflat, n_chunks_total=4):
    outs = []
    for core in range(N_CORES):
        s = core * TOK_PER_CORE
        blk = np.zeros((HALO + TOK_PER_CORE, D), np.float32)
        if s % SEQ == 0:
            blk[HALO:] = xflat[s:s + TOK_PER_CORE]
        else:
            blk[:] = xflat[s - HALO:s + TOK_PER_CORE]
        outs.append(np.ascontiguousarray(blk.T.reshape(NF, 128, -1).transpose(1, 0, 2)))
    return outs


def _unshard_y(res):
    outs = []
    for r in res:
        yT = np.asarray(r["yT"])
        outs.append(yT.transpose(2, 1, 0).reshape(TOK_PER_CORE, D))
    return np.concatenate(outs, axis=0)


_NC_CACHE = {}


def _get_nc(n_layers, n_chunks):
    key = (n_layers, n_chunks)
    if key not in _NC_CACHE:
        _NC_CACHE[key] = build(n_layers, n_chunks)
    return _NC_CACHE[key]


FUSED = True


def kernel(**inputs):
    p = {k: np.asarray(v, dtype=np.float32) for k, v in inputs.items()}
    x = p["x"]
    B, S, _ = x.shape
    xflat = x.reshape(B * S, D)
    consts = [_core_consts(c) for c in range(N_CORES)]
    if FUSED:
        nc = _get_nc(DEPTH, 4)
        w = _layer_weights(p, range(DEPTH))
        xs = _shard_x(xflat)
        in_maps = [dict(xT=xs[c], cvec=consts[c], **w) for c in range(N_CORES)]
        res = run_bass_kernel_spmd(nc, in_maps, core_ids=list(range(N_CORES)))
        xflat = _unshard_y(res.results)
    else:
        nc = _get_nc(1, 4)
        for l in range(DEPTH):
            w = _layer_weights(p, [l])
            xs = _shard_x(xflat)
            in_maps = [dict(xT=xs[c], cvec=consts[c], **w) for c in range(N_CORES)]
            res = run_bass_kernel_spmd(nc, in_maps, core_ids=list(range(N_CORES)))
            xflat = _unshard_y(res.results)
    return xflat.reshape(B, S, D).astype(np.float32)
```

```python
import contextlib
import numpy as np
import concourse.bass as bass
import concourse.mybir as mybir
from concourse.bass_utils import run_bass_kernel_spmd

F32 = mybir.dt.float32
BF16 = mybir.dt.bfloat16
AF = mybir.ActivationFunctionType
ALU = mybir.AluOpType

N_CORES = 8
D = 2048
NF = 16
NFF = 44
DEPTH = 4
SEQ = 8192
TOK_PER_CORE = 2048
T = 640
PADL = 32
CH = 512
HALO = 128
WSLOT = 2816
NSLOT = 6
NSCR = 4
NRS = 4

NPL = 642
OFF_GMP, OFF_GMO, OFF_GFP, OFF_GFO = 0, 16, 32, 48
OFF_CA, OFF_CB, OFF_CBB, OFF_LNG, OFF_LNB, OFF_PS, OFF_CF, OFF_CFB = 64, 82, 268, 274, 280, 286, 290, 554
POOL_W = (2, 4, 8, 16)


class Sem:
    def __init__(self, h):
        self.h = h
        self.cnt = 0


class Eng:
    def __init__(self, name, e, sem):
        self.name = name
        self.e = e
        self.sem = sem
        self.seen = {}


class Res:
    __slots__ = ("w", "r")

    def __init__(self):
        self.w = None
        self.r = {}


def tiles_of(r0):
    n = T - r0
    h = ((n + 1) // 2 + 1) // 2 * 2
    return [(r0, r0 + h), (r0 + h, T)]


def build(n_layers, n_chunks, n_ranks=N_CORES):
    nc = bass.Bass("TRN2", target_bir_lowering=False)
    NT = HALO + CH * n_chunks
    xT_d = nc.dram_tensor("xT", [128, NF, NT], F32, kind="ExternalInput").ap()
    yT_d = nc.dram_tensor("yT", [128, NF, CH * n_chunks], F32, kind="ExternalOutput").ap()
    win_d = nc.dram_tensor("win", [n_layers, 34, 128, 2048], F32, kind="ExternalInput").ap()
    wout_d = nc.dram_tensor("wout", [n_layers, 16, 128, 2048], F32, kind="ExternalInput").ap()
    wup_d = nc.dram_tensor("wup", [n_layers, 88, 128, 2048], F32, kind="ExternalInput").ap()
    wdn_d = nc.dram_tensor("wdn", [n_layers, 32, 128, WSLOT], F32, kind="ExternalInput").ap()
    pool_d = nc.dram_tensor("poolw", [128, n_layers * 4 * 128], F32, kind="ExternalInput").ap()
    pvec_d = nc.dram_tensor("pvec", [128, n_layers * NPL], F32, kind="ExternalInput").ap()
    cvec_d = nc.dram_tensor("cvec", [128, 65], F32, kind="ExternalInput").ap()

    with contextlib.ExitStack() as st:
        def sb(name, shape, dt):
            return st.enter_context(nc.sbuf_tensor(name, shape, dt))

        def mksem(name):
            return Sem(st.enter_context(nc.semaphore(name)))

        xT = sb("xT_sb", [128, NF * T], F32)
        oT = sb("oT_sb", [128, NF * T], F32)
        hT = oT.bitcast(BF16)
        big = sb("big_sb", [128, NFF * T], BF16)
        bigf = big.bitcast(F32)
        wring = [sb(f"wring{i}", [128, WSLOT], BF16) for i in range(NSLOT)]
        scr = [sb(f"scr{i}", [128, T], F32) for i in range(NSCR)]
        rsb = [sb(f"rsb{i}", [128, T], F32) for i in range(NRS)]
        scrb = [t_.bitcast(BF16) for t_ in scr]
        pvec = sb("pvec_sb", [128, n_layers * NPL], F32)
        poolw = sb("poolw_sb", [128, n_layers * 4 * 128], BF16)
        cvec = sb("cvec_sb", [128, 65], F32)
        ones = sb("ones_sb", [128, 128], F32)
        onesb = sb("onesb_sb", [128, 128], BF16)
        save_hm = sb("save_hm", [128, n_layers * NF * 30], BF16)
        save_hf = sb("save_hf", [128, n_layers * NF * 2], BF16)
        epsb = sb("eps_sb", [128, 2], F32)
        ps = st.enter_context(nc.psum_tensor("ps", [128, 8 * 512], F32))

        PE = Eng("pe", nc.tensor, mksem("s_pe"))
        ACT = Eng("act", nc.scalar, mksem("s_act"))
        DVE = Eng("dve", nc.vector, mksem("s_dve"))
        POOL = Eng("pool", nc.gpsimd, mksem("s_pool"))
        SP = Eng("sp", nc.sync, mksem("s_sp"))
        wsem = [mksem(f"s_w{i}") for i in range(NSLOT)]
        xsem = mksem("s_x")
        ysem = mksem("s_y")
        csem = [mksem(f"s_c{i}") for i in range(3)]

        block = st.enter_context(nc.Block())

        R_x = [Res() for _ in range(NF)]
        R_o = [Res() for _ in range(NF)]
        R_y = [Res() for _ in range(NF)]
        R_yb = [Res() for _ in range(6)]
        R_cv = [Res(), Res()]
        R_glu = [Res(), Res()]
        R_uc = Res()
        R_s = [Res(), Res()]
        R_act = [Res() for _ in range(NFF)]
        R_w = [Res() for _ in range(NSLOT)]
        R_scr = [Res() for _ in range(NSCR)]
        R_rs = [Res() for _ in range(NRS)]
        R_ps = [Res() for _ in range(8)]
        R_pv, R_pw, R_cvec, R_ones, R_eps = Res(), Res(), Res(), Res(), Res()
        R_shm = [Res() for _ in range(n_layers)]
        R_shf = [Res() for _ in range(n_layers)]

        def _deps(E, reads, writes, pe):
            deps = {}

            def add(s, v):
                if pe and s is E.sem:
                    return
                if deps.get(s, 0) < v:
                    deps[s] = v
            for r in reads:
                if r.w is not None:
                    add(*r.w)
            for w in writes:
                if w.w is not None:
                    add(*w.w)
                for s, v in w.r.items():
                    add(s, v)
            for s, v in deps.items():
                if E.seen.get(s, 0) < v:
                    E.e.wait_ge(s.h, v)
                    E.seen[s] = v

        def _commit(tok, reads, writes):
            s, v = tok
            for r in reads:
                if r.r.get(s, 0) < v:
                    r.r[s] = v
            for w in writes:
                w.w = tok
                w.r = {}

        def op(E, fn, reads=(), writes=(), pe=False):
            _deps(E, reads, writes, pe)
            ins = fn()
            E.sem.cnt += 1
            ins.then_inc(E.sem.h, 1)
            tok = (E.sem, E.sem.cnt)
            _commit(tok, reads, writes)
            return tok

        def dma(E, fn, sem, reads=(), writes=()):
            _deps(E, reads, writes, False)
            ins = fn()
            sem.cnt += 16
            ins.then_inc(sem.h, 16)
            tok = (sem, sem.cnt)
            _commit(tok, reads, writes)
            return tok

        state = {"bank": 7, "scr": 0, "rs": 0, "slot": 0, "pinned": set()}

        def alloc_bank():
            while True:
                state["bank"] = (state["bank"] + 1) % 8
                if state["bank"] not in state["pinned"]:
                    return state["bank"]

        def nscr():
            state["scr"] = (state["scr"] + 1) % NSCR
            return state["scr"]

        def nrs():
            state["rs"] = (state["rs"] + 1) % NRS
            return state["rs"]

        def PS(b, c0, c1):
            return ps[:, b * 512 + c0: b * 512 + c1]

        def X(ft, t0, t1):
            return xT[:, ft * T + t0: ft * T + t1]

        def H(ft, t0, t1):
            return hT[:, ft * T + t0: ft * T + t1]

        def O(m, t0, t1):
            return oT[:, m * T + t0: m * T + t1]

        def Y(k, t0, t1):
            return big[:, k * T + t0: k * T + t1]

        def A(j, t0, t1):
            return big[:, j * T + t0: j * T + t1]

        YB0 = 8 * T
        PB = PADL + T
        CV0 = 14 * T

        def YB(j, t0, t1):
            return bigf[:, YB0 + j * T + t0: YB0 + j * T + t1]

        def padbuf(idx):
            base = CV0 + idx * PB + PADL

            def f(t0, t1):
                return bigf[:, base + t0: base + t1]
            return f
        CV = [padbuf(0), padbuf(1)]
        GLU = [padbuf(2), padbuf(3)]
        UC = padbuf(4)
        SS = [padbuf(5), padbuf(6)]

        def pv(l, off, n=1):
            return pvec[:, l * NPL + off: l * NPL + off + n]

        act = nc.scalar.activation
        V = nc.vector

        def load_w(src_ap, nelem):
            s = state["slot"]
            state["slot"] = (s + 1) % NSLOT
            dma(POOL, lambda: nc.gpsimd.dma_start(out=wring[s][:, 0:nelem], in_=src_ap), wsem[s],
                writes=[R_w[s]])
            return s

        def mm_group(bank, n, pairs, reads):
            def fn():
                ins = None
                last = len(pairs) - 1
                for i, (l_, r_) in enumerate(pairs):
                    ins = nc.tensor.matmul(PS(bank, 0, n), l_, r_, start=(i == 0), stop=(i == last))
                return ins
            return op(PE, fn, reads=reads, writes=[R_ps[bank]], pe=True)

        def ones_mm(bank, n, rhs_ap, rhs_res, first, last):
            return op(PE, lambda: nc.tensor.matmul(PS(bank, 0, n), ones[:, :], rhs_ap, start=first, stop=last),
                      reads=[rhs_res, R_ones], writes=[R_ps[bank]], pe=True)

        def ones_mm_bf(bank, n, rhs_ap, rhs_res, first, last):
            return op(PE, lambda: nc.tensor.matmul(PS(bank, 0, n), onesb[:, :], rhs_ap, start=first, stop=last),
                      reads=[rhs_res, R_ones], writes=[R_ps[bank]], pe=True)

        def proj(slot, rhs_fn, rhs_res, w0, t1):
            bank = alloc_bank()
            pairs = [(wring[slot][:, kt * 128:(kt + 1) * 128], rhs_fn(kt, w0, t1)) for kt in range(16)]
            mm_group(bank, t1 - w0, pairs, [R_w[slot]] + rhs_res)
            return bank

        def finish_rstd(banks, tl, r0, dim, epscol, chunk, use_mask):
            rs = nrs()
            for ti, (t0, t1) in enumerate(tl):
                n = t1 - t0
                op(ACT, lambda: act(out=rsb[rs][:, t0:t1], in_=PS(banks[ti], 0, n), func=AF.Sqrt,
                                    bias=epsb[:, epscol:epscol + 1], scale=1.0 / dim),
                   reads=[R_ps[banks[ti]], R_eps], writes=[R_rs[rs]])
            op(DVE, lambda: V.reciprocal(out=rsb[rs][:, r0:T], in_=rsb[rs][:, r0:T]),
               reads=[R_rs[rs]], writes=[R_rs[rs]])
            if use_mask and chunk == 0 and r0 < HALO:
                op(DVE, lambda: V.tensor_scalar(out=rsb[rs][:, r0:HALO], in0=rsb[rs][:, r0:HALO],
                                                scalar1=cvec[:, 0:1], scalar2=None, op0=ALU.mult),
                   reads=[R_rs[rs], R_cvec], writes=[R_rs[rs]])
            for b in banks:
                state["pinned"].discard(b)
            return rs

        def pre_norm(l, r0, goff):
            tl = tiles_of(r0)
            banks = [alloc_bank(), alloc_bank()]
            state["pinned"].update(banks)
            for ft in range(NF):
                sc = nscr()
                op(ACT, lambda: act(out=scrb[sc][:, r0:T], in_=X(ft, r0, T), func=AF.Square),
                   reads=[R_x[ft]], writes=[R_scr[sc]])
                for ti, (t0, t1) in enumerate(tl):
                    ones_mm_bf(banks[ti], t1 - t0, scrb[sc][:, t0:t1], R_scr[sc], ft == 0, ft == NF - 1)
            rs = finish_rstd(banks, tl, r0, float(D), 0, None, False)
            for ft in range(NF):
                op(DVE, lambda: V.scalar_tensor_tensor(out=H(ft, r0, T), in0=X(ft, r0, T),
                                                       scalar=pv(l, goff + ft), in1=rsb[rs][:, r0:T],
                                                       op0=ALU.mult, op1=ALU.mult),
                   reads=[R_x[ft], R_rs[rs], R_pv], writes=[R_o[ft // 2]])

        def post_norm_residual(l, r0, goff, banks, tl, chunk):
            rs = finish_rstd(banks, tl, r0, float(D), 0, chunk, True)
            for m in range(NF):
                op(DVE, lambda: V.scalar_tensor_tensor(out=O(m, r0, T), in0=O(m, r0, T),
                                                       scalar=pv(l, goff + m), in1=rsb[rs][:, r0:T],
                                                       op0=ALU.mult, op1=ALU.mult),
                   reads=[R_o[m], R_rs[rs], R_pv], writes=[R_o[m]])
                op(DVE, lambda: V.tensor_tensor(out=X(m, r0, T), in0=X(m, r0, T), in1=O(m, r0, T), op=ALU.add),
                   reads=[R_x[m], R_o[m]], writes=[R_x[m]])

        def evac_with_sq(bank, n, m, t0, t1, ssbank, first, last):
            op(DVE, lambda: V.tensor_copy(out=O(m, t0, t1), in_=PS(bank, 0, n)),
               reads=[R_ps[bank]], writes=[R_o[m]])

        def o_stats(r0, banks, tl):
            for m in range(NF):
                sc = nscr()
                op(ACT, lambda: act(out=scrb[sc][:, r0:T], in_=O(m, r0, T), func=AF.Square),
                   reads=[R_o[m]], writes=[R_scr[sc]])
                for ti, (t0, t1) in enumerate(tl):
                    ones_mm_bf(banks[ti], t1 - t0, scrb[sc][:, t0:t1], R_scr[sc], m == 0, m == NF - 1)

        def layer(l, chunk):
            first = (chunk == 0)
            last_chunk = (chunk == n_chunks - 1)
            if first:
                a = 32 * l
                b = 32 * l + 28
                hm0, hf0 = a, b - 2
            else:
                a = HALO - 30
                b = HALO
                hm0, hf0 = HALO, HALO
            tla = tiles_of(a)
            tlb = tiles_of(b)
            hres = [R_o[i] for i in range(8)]
            hT3 = hT[:, 0:NF * T].rearrange("p (f t) -> p f t", t=T)
            shm3 = save_hm[:, l * NF * 30:(l + 1) * NF * 30].rearrange("p (f c) -> p f c", c=30)
            shf3 = save_hf[:, l * NF * 2:(l + 1) * NF * 2].rearrange("p (f c) -> p f c", c=2)

            pre_norm(l, hm0, OFF_GMP)
            if not first:
                op(DVE, lambda: V.tensor_copy(out=hT3[:, :, HALO - 30:HALO], in_=shm3),
                   reads=[R_shm[l]], writes=hres)
            if not last_chunk:
                op(DVE, lambda: V.tensor_copy(out=shm3, in_=hT3[:, :, T - 30:T]),
                   reads=hres, writes=[R_shm[l]])

            def b_proj(j):
                sv = load_w(win_d[l, 18 + j], 2048)
                sg = load_w(win_d[l, 24 + j], 2048)
                gi = j % 2
                for (t0, t1) in tla:
                    n = t1 - t0
                    bv = proj(sv, H, hres, t0, t1)
                    bg = proj(sg, H, hres, t0, t1)
                    sc = nscr()
                    op(ACT, lambda: act(out=scr[sc][:, 0:n], in_=PS(bg, 0, n), func=AF.Sigmoid),
                       reads=[R_ps[bg]], writes=[R_scr[sc]])
                    op(DVE, lambda: V.tensor_tensor(out=GLU[gi](t0, t1), in0=scr[sc][:, 0:n], in1=PS(bv, 0, n),
                                                    op=ALU.mult),
                       reads=[R_scr[sc], R_ps[bv]], writes=[R_glu[gi]])

            def b_conv(j):
                gi = j % 2
                op(ACT, lambda: act(out=YB(j, b, T), in_=GLU[gi](b - 30, T - 30), func=AF.Identity,
                                    bias=pv(l, OFF_CBB + j), scale=pv(l, OFF_CB + j * 31)),
                   reads=[R_glu[gi], R_pv], writes=[R_yb[j]])
                for k in range(1, 31):
                    op(DVE, lambda: V.scalar_tensor_tensor(out=YB(j, b, T), in0=GLU[gi](b - 30 + k, T - 30 + k),
                                                           scalar=pv(l, OFF_CB + j * 31 + k), in1=YB(j, b, T),
                                                           op0=ALU.mult, op1=ALU.add),
                       reads=[R_glu[gi], R_pv, R_yb[j]], writes=[R_yb[j]])

            def a_head(j):
                sb_ = load_w(win_d[l, j], 2048)
                sc_ = load_w(win_d[l, 6 + j], 2048)
                sv_ = load_w(win_d[l, 12 + j], 2048)
                ci = j % 2
                for (t0, t1) in tla:
                    n = t1 - t0
                    bb = proj(sb_, H, hres, t0, t1)
                    bc = proj(sc_, H, hres, t0, t1)
                    bv = proj(sv_, H, hres, t0, t1)
                    s1 = nscr()
                    op(ACT, lambda: act(out=scr[s1][:, 0:n], in_=PS(bc, 0, n), func=AF.Copy),
                       reads=[R_ps[bc]], writes=[R_scr[s1]])
                    op(DVE, lambda: V.tensor_tensor(out=CV[ci](t0, t1), in0=scr[s1][:, 0:n], in1=PS(bv, 0, n),
                                                    op=ALU.mult),
                       reads=[R_scr[s1], R_ps[bv]], writes=[R_cv[ci]])
                    s2 = nscr()
                    op(ACT, lambda: act(out=scr[s2][:, 0:n], in_=CV[ci](t0 - 2, t1 - 2), func=AF.Identity,
                                        scale=pv(l, OFF_CA + j * 3)),
                       reads=[R_cv[ci], R_pv], writes=[R_scr[s2]])
                    for k in (1, 2):
                        op(DVE, lambda: V.scalar_tensor_tensor(out=scr[s2][:, 0:n], in0=CV[ci](t0 - 2 + k, t1 - 2 + k),
                                                               scalar=pv(l, OFF_CA + j * 3 + k), in1=scr[s2][:, 0:n],
                                                               op0=ALU.mult, op1=ALU.add),
                           reads=[R_cv[ci], R_pv, R_scr[s2]], writes=[R_scr[s2]])
                    op(DVE, lambda: V.tensor_tensor(out=Y(j, t0, t1), in0=scr[s2][:, 0:n], in1=PS(bb, 0, n),
                                                    op=ALU.mult),
                       reads=[R_scr[s2], R_ps[bb]], writes=[R_y[j]])


            for j in range(6):
                b_proj(j)
                a_head(j)
                b_conv(j)

            for g in range(4):
                su = load_w(win_d[l, 30 + g], 2048)
                w = POOL_W[g]
                for (t0, t1) in tla:
                    n = t1 - t0
                    bu = proj(su, H, hres, t0, t1)
                    op(ACT, lambda: act(out=UC(t0, t1), in_=PS(bu, 0, n), func=AF.Copy),
                       reads=[R_ps[bu]], writes=[R_uc])
                src, src_res = UC, R_uc
                for k in range(1, g + 2):
                    sh = 1 << (k - 1)
                    lo = a - (w - (1 << k))
                    di = (k - 1) % 2
                    dst, dst_res = SS[di], R_s[di]
                    op(DVE, lambda: V.tensor_tensor(out=dst(lo, T), in0=src(lo, T), in1=src(lo - sh, T - sh),
                                                    op=ALU.add),
                       reads=[src_res], writes=[dst_res])
                    src, src_res = dst, dst_res
                op(DVE, lambda: V.scalar_tensor_tensor(out=Y(12 + g, a, T), in0=src(a, T), scalar=1.0 / w,
                                                       in1=UC(a, T), op0=ALU.mult, op1=ALU.subtract),
                   reads=[src_res, R_uc], writes=[R_y[12 + g]])
                if chunk == 0:
                    sc = nscr()
                    op(DVE, lambda: V.tensor_tensor(out=scr[sc][:, 0:16], in0=src(HALO, HALO + 16),
                                                    in1=cvec[:, 1 + g * 16: 17 + g * 16], op=ALU.mult),
                       reads=[src_res, R_cvec], writes=[R_scr[sc]])
                    op(DVE, lambda: V.tensor_tensor(out=Y(12 + g, HALO, HALO + 16), in0=scr[sc][:, 0:16],
                                                    in1=UC(HALO, HALO + 16), op=ALU.subtract),
                       reads=[R_scr[sc], R_uc], writes=[R_y[12 + g]])
                for (t0, t1) in tla:
                    n = t1 - t0
                    bm = alloc_bank()
                    pw_ap = poolw[:, (l * 4 + g) * 128:(l * 4 + g + 1) * 128]
                    op(PE, lambda: nc.tensor.matmul(PS(bm, 0, n), pw_ap, Y(12 + g, t0, t1), start=True, stop=True),
                       reads=[R_pw, R_y[12 + g]], writes=[R_ps[bm]], pe=True)
                    op(DVE, lambda: V.tensor_scalar(out=Y(12 + g, t0, t1), in0=PS(bm, 0, n),
                                                    scalar1=pv(l, OFF_PS + g), scalar2=None, op0=ALU.mult),
                       reads=[R_ps[bm], R_pv], writes=[R_y[12 + g]])

            b1 = [alloc_bank(), alloc_bank()]
            b2 = [alloc_bank(), alloc_bank()]
            state["pinned"].update(b1 + b2)
            for j in range(6):
                for ti, (t0, t1) in enumerate(tlb):
                    ones_mm(b1[ti], t1 - t0, YB(j, t0, t1), R_yb[j], j == 0, j == 5)
            for j in range(6):
                sc = nscr()
                op(ACT, lambda: act(out=scrb[sc][:, b:T], in_=YB(j, b, T), func=AF.Square),
                   reads=[R_yb[j]], writes=[R_scr[sc]])
                for ti, (t0, t1) in enumerate(tlb):
                    ones_mm_bf(b2[ti], t1 - t0, scrb[sc][:, t0:t1], R_scr[sc], j == 0, j == 5)
            rmean, rvar, rnmr = nrs(), nrs(), nrs()
            for ti, (t0, t1) in enumerate(tlb):
                n = t1 - t0
                op(DVE, lambda: V.tensor_scalar(out=rsb[rmean][:, t0:t1], in0=PS(b1[ti], 0, n),
                                                scalar1=1.0 / 768.0, scalar2=None, op0=ALU.mult),
                   reads=[R_ps[b1[ti]]], writes=[R_rs[rmean]])
            sc = nscr()
            op(DVE, lambda: V.tensor_tensor(out=scr[sc][:, b:T], in0=rsb[rmean][:, b:T], in1=rsb[rmean][:, b:T],
                                            op=ALU.mult),
               reads=[R_rs[rmean]], writes=[R_scr[sc]])
            for ti, (t0, t1) in enumerate(tlb):
                n = t1 - t0
                op(DVE, lambda: V.scalar_tensor_tensor(out=rsb[rvar][:, t0:t1], in0=PS(b2[ti], 0, n),
                                                       scalar=1.0 / 768.0, in1=scr[sc][:, t0:t1],
                                                       op0=ALU.mult, op1=ALU.subtract),
                   reads=[R_ps[b2[ti]], R_scr[sc]], writes=[R_rs[rvar]])
            for bb in b1 + b2:
                state["pinned"].discard(bb)
            op(ACT, lambda: act(out=rsb[rvar][:, b:T], in_=rsb[rvar][:, b:T], func=AF.Sqrt,
                                bias=epsb[:, 1:2], scale=1.0),
               reads=[R_rs[rvar], R_eps], writes=[R_rs[rvar]])
            op(DVE, lambda: V.reciprocal(out=rsb[rvar][:, b:T], in_=rsb[rvar][:, b:T]),
               reads=[R_rs[rvar]], writes=[R_rs[rvar]])
            op(DVE, lambda: V.scalar_tensor_tensor(out=rsb[rnmr][:, b:T], in0=rsb[rmean][:, b:T], scalar=-1.0,
                                                   in1=rsb[rvar][:, b:T], op0=ALU.mult, op1=ALU.mult),
               reads=[R_rs[rmean], R_rs[rvar]], writes=[R_rs[rnmr]])
            for j in range(6):
                sc = nscr()
                op(DVE, lambda: V.tensor_tensor(out=scr[sc][:, b:T], in0=YB(j, b, T), in1=rsb[rvar][:, b:T],
                                                op=ALU.mult),
                   reads=[R_yb[j], R_rs[rvar]], writes=[R_scr[sc]])
                op(DVE, lambda: V.tensor_tensor(out=scr[sc][:, b:T], in0=scr[sc][:, b:T], in1=rsb[rnmr][:, b:T],
                                                op=ALU.add),
                   reads=[R_scr[sc], R_rs[rnmr]], writes=[R_scr[sc]])
                op(ACT, lambda: act(out=Y(6 + j, b, T), in_=scr[sc][:, b:T], func=AF.Silu,
                                    bias=pv(l, OFF_LNB + j), scale=pv(l, OFF_LNG + j)),
                   reads=[R_scr[sc], R_pv], writes=[R_y[6 + j]])

            bss = [alloc_bank(), alloc_bank()]
            state["pinned"].update(bss)
            for m in range(NF):
                s = load_w(wout_d[l, m], 2048)
                for ti, (t0, t1) in enumerate(tlb):
                    bank = proj(s, Y, R_y, t0, t1)
                    evac_with_sq(bank, t1 - t0, m, t0, t1, bss[ti], m == 0, m == NF - 1)
            o_stats(b, bss, tlb)
            post_norm_residual(l, b, OFF_GMO, bss, tlb, chunk)

            pre_norm(l, hf0, OFF_GFP)
            if not first:
                op(DVE, lambda: V.tensor_copy(out=hT3[:, :, HALO - 2:HALO], in_=shf3),
                   reads=[R_shf[l]], writes=hres)
            if not last_chunk:
                op(DVE, lambda: V.tensor_copy(out=shf3, in_=hT3[:, :, T - 2:T]),
                   reads=hres, writes=[R_shf[l]])
            for j in range(NFF):
                sg = load_w(wup_d[l, j], 2048)
                sv = load_w(wup_d[l, NFF + j], 2048)
                for (t0, t1) in tlb:
                    n = t1 - t0
                    w0 = t0 - 2
                    bg = proj(sg, H, hres, w0, t1)
                    bv = proj(sv, H, hres, w0, t1)
                    accs = []
                    for (bank, m) in ((bg, j), (bv, NFF + j)):
                        sc = nscr()
                        op(DVE, lambda: V.tensor_scalar(out=scr[sc][:, 0:n], in0=PS(bank, 2, n + 2),
                                                        scalar1=pv(l, OFF_CF + m * 3 + 2), scalar2=pv(l, OFF_CFB + m),
                                                        op0=ALU.mult, op1=ALU.add),
                           reads=[R_ps[bank], R_pv], writes=[R_scr[sc]])
                        for k in (1, 0):
                            op(DVE, lambda: V.scalar_tensor_tensor(out=scr[sc][:, 0:n], in0=PS(bank, k, n + k),
                                                                   scalar=pv(l, OFF_CF + m * 3 + k),
                                                                   in1=scr[sc][:, 0:n], op0=ALU.mult, op1=ALU.add),
                               reads=[R_ps[bank], R_pv, R_scr[sc]], writes=[R_scr[sc]])
                        accs.append(sc)
                    sg_, sv_ = accs
                    op(ACT, lambda: act(out=scr[sg_][:, 0:n], in_=scr[sg_][:, 0:n], func=AF.Silu),
                       reads=[R_scr[sg_]], writes=[R_scr[sg_]])
                    op(DVE, lambda: V.tensor_tensor(out=A(j, t0, t1), in0=scr[sg_][:, 0:n], in1=scr[sv_][:, 0:n],
                                                    op=ALU.mult),
                       reads=[R_scr[sg_], R_scr[sv_]], writes=[R_act[j]])

            bss = [alloc_bank(), alloc_bank()]
            state["pinned"].update(bss)
            for m in range(NF):
                s0 = load_w(wdn_d[l, 2 * m], WSLOT)
                s1 = load_w(wdn_d[l, 2 * m + 1], WSLOT)
                for ti, (t0, t1) in enumerate(tlb):
                    bank = alloc_bank()
                    pairs = []
                    for kt in range(NFF):
                        ws = wring[s0] if kt < 22 else wring[s1]
                        kk = kt % 22
                        pairs.append((ws[:, kk * 128:(kk + 1) * 128], A(kt, t0, t1)))
                    mm_group(bank, t1 - t0, pairs, [R_w[s0], R_w[s1]] + R_act)
                    evac_with_sq(bank, t1 - t0, m, t0, t1, bss[ti], m == 0, m == NF - 1)
            o_stats(b, bss, tlb)
            post_norm_residual(l, b, OFF_GFO, bss, tlb, chunk)

        dma(SP, lambda: nc.sync.dma_start(out=pvec[:, :], in_=pvec_d), csem[0], writes=[R_pv])
        dma(SP, lambda: nc.sync.dma_start(out=cvec[:, :], in_=cvec_d), csem[1], writes=[R_cvec])
        dma(POOL, lambda: nc.gpsimd.dma_start(out=poolw[:, :], in_=pool_d), csem[2], writes=[R_pw])
        op(DVE, lambda: V.memset(ones[:, :], 1.0), writes=[R_ones])
        op(DVE, lambda: V.memset(onesb[:, :], 1.0), writes=[R_ones])
        op(DVE, lambda: V.memset(epsb[:, 0:1], 1e-6), writes=[R_eps])
        op(DVE, lambda: V.memset(epsb[:, 1:2], 1e-5), writes=[R_eps])
        op(DVE, lambda: V.memset(oT[:, :], 0.0), writes=R_o)
        op(DVE, lambda: V.memset(bigf[:, :], 0.0),
           writes=R_y + R_yb + R_cv + R_glu + [R_uc] + R_s + R_act)
        for i in range(NSCR):
            op(DVE, lambda: V.memset(scr[i][:, :], 0.0), writes=[R_scr[i]])
        for i in range(NRS):
            op(DVE, lambda: V.memset(rsb[i][:, :], 0.0), writes=[R_rs[i]])

        xT3 = xT[:, :].rearrange("p (f t) -> p f t", t=T)
        for c in range(n_chunks):
            dma(SP, lambda: nc.sync.dma_start(out=xT3, in_=xT_d[:, :, c * CH: c * CH + T]), xsem, writes=R_x)
            for l in range(n_layers):
                layer(l, c)
            dma(SP, lambda: nc.sync.dma_start(out=yT_d[:, :, c * CH:(c + 1) * CH], in_=xT3[:, :, HALO:T]),
                ysem, reads=R_x)
        nc.sync.wait_ge(ysem.h, ysem.cnt)
    return nc


def _fm(vec):
    return np.ascontiguousarray(vec.reshape(-1, 128).T)


def _pack_layer_params(p, l):
    cols = [
        _fm(p["norm_mix_pre"][l]), _fm(p["norm_mix_post"][l]),
        _fm(p["norm_ffn_pre"][l]), _fm(p["norm_ffn_post"][l]),
        p["conv_a_w"][l].reshape(3, 6, 128).transpose(2, 1, 0).reshape(128, 18),
        p["conv_b_w"][l].reshape(31, 6, 128).transpose(2, 1, 0).reshape(128, 186),
        _fm(p["conv_b_bias"][l]), _fm(p["ln_b_gain"][l]), _fm(p["ln_b_bias"][l]),
        _fm(p["pool_scale"][l]),
        p["conv_ffn_w"][l].reshape(3, 88, 128).transpose(2, 1, 0).reshape(128, 264),
        _fm(p["conv_ffn_bias"][l]),
    ]
    out = np.concatenate(cols, axis=1).astype(np.float32)
    assert out.shape == (128, NPL)
    return out


def _tile_w(w, n_m):
    return np.ascontiguousarray(w.reshape(16, 128, n_m, 128).transpose(2, 1, 0, 3).reshape(n_m, 128, 2048))


def _tile_wdn(w):
    return np.ascontiguousarray(
        w.reshape(2, 22, 128, 16, 128).transpose(3, 0, 2, 1, 4).reshape(32, 128, WSLOT))


def _layer_weights(p, layers):
    win = np.stack([_tile_w(p["w_in"][l], 34) for l in layers])
    wout = np.stack([_tile_w(p["w_out"][l], 16) for l in layers])
    wup = np.stack([_tile_w(p["w_up"][l], 88) for l in layers])
    wdn = np.stack([_tile_wdn(p["w_down"][l]) for l in layers])
    poolw = np.concatenate([p["pool_w"][l].transpose(1, 0, 2).reshape(128, 4 * 128) for l in layers], axis=1)
    pvec = np.concatenate([_pack_layer_params(p, l) for l in layers], axis=1)
    return dict(win=win, wout=wout, wup=wup, wdn=wdn,
                poolw=np.ascontiguousarray(poolw.astype(np.float32)),
                pvec=np.ascontiguousarray(pvec))


def _core_consts(core):
    cv = np.zeros((128, 65), np.float32)
    start = (core * TOK_PER_CORE) % SEQ == 0
    cv[:, 0] = 0.0 if start else 1.0
    for g, w in enumerate(POOL_W):
        for i in range(16):
            cv[:, 1 + g * 16 + i] = 1.0 / (min(i + 1, w) if start else w)
    return cv


def _shard_x(xflat, n_chunks_total=4):
    outs = []
    for core in range(N_CORES):
        s = core * TOK_PER_CORE
        blk = np.zeros((HALO + TOK_PER_CORE, D), np.float32)
        if s % SEQ == 0:
            blk[HALO:] = xflat[s:s + TOK_PER_CORE]
        else:
            blk[:] = xflat[s - HALO:s + TOK_PER_CORE]
        outs.append(np.ascontiguousarray(blk.T.reshape(NF, 128, -1).transpose(1, 0, 2)))
    return outs


def _unshard_y(res):
    outs = []
    for r in res:
        yT = np.asarray(r["yT"])
        outs.append(yT.transpose(2, 1, 0).reshape(TOK_PER_CORE, D))
    return np.concatenate(outs, axis=0)


_NC_CACHE = {}


def _get_nc(n_layers, n_chunks):
    key = (n_layers, n_chunks)
    if key not in _NC_CACHE:
        _NC_CACHE[key] = build(n_layers, n_chunks)
    return _NC_CACHE[key]


FUSED = True


def kernel(**inputs):
    p = {k: np.asarray(v, dtype=np.float32) for k, v in inputs.items()}
    x = p["x"]
    B, S, _ = x.shape
    xflat = x.reshape(B * S, D)
    consts = [_core_consts(c) for c in range(N_CORES)]
    if FUSED:
        nc = _get_nc(DEPTH, 4)
        w = _layer_weights(p, range(DEPTH))
        xs = _shard_x(xflat)
        in_maps = [dict(xT=xs[c], cvec=consts[c], **w) for c in range(N_CORES)]
        res = run_bass_kernel_spmd(nc, in_maps, core_ids=list(range(N_CORES)))
        xflat = _unshard_y(res.results)
    else:
        nc = _get_nc(1, 4)
        for l in range(DEPTH):
            w = _layer_weights(p, [l])
            xs = _shard_x(xflat)
            in_maps = [dict(xT=xs[c], cvec=consts[c], **w) for c in range(N_CORES)]
            res = run_bass_kernel_spmd(nc, in_maps, core_ids=list(range(N_CORES)))
            xflat = _unshard_y(res.results)
    return xflat.reshape(B, S, D).astype(np.float32)
```

```python
import contextlib
import numpy as np
import concourse.bass as bass
import concourse.mybir as mybir
from concourse.bass_utils import run_bass_kernel_spmd

F32 = mybir.dt.float32
BF16 = mybir.dt.bfloat16
AF = mybir.ActivationFunctionType
ALU = mybir.AluOpType

N_CORES = 8
D = 2048
NF = 16
NFF = 44
DEPTH = 4
SEQ = 8192
TOK_PER_CORE = 2048
T = 640
PADL = 32
CH = 512
HALO = 128
WSLOT = 2816
NSLOT = 6
NSCR = 4
NRS = 4

NPL = 642
OFF_GMP, OFF_GMO, OFF_GFP, OFF_GFO = 0, 16, 32, 48
OFF_CA, OFF_CB, OFF_CBB, OFF_LNG, OFF_LNB, OFF_PS, OFF_CF, OFF_CFB = 64, 82, 268, 274, 280, 286, 290, 554
POOL_W = (2, 4, 8, 16)


class Sem:
    def __init__(self, h):
        self.h = h
        self.cnt = 0


class Eng:
    def __init__(self, name, e, sem):
        self.name = name
        self.e = e
        self.sem = sem
        self.seen = {}


class Res:
    __slots__ = ("w", "r")

    def __init__(self):
        self.w = None
        self.r = {}


def tiles_of(r0):
    n = T - r0
    h = ((n + 1) // 2 + 1) // 2 * 2
    return [(r0, r0 + h), (r0 + h, T)]


def build(n_layers, n_chunks, n_ranks=N_CORES):
    nc = bass.Bass("TRN2", target_bir_lowering=False)
    NT = HALO + CH * n_chunks
    xT_d = nc.dram_tensor("xT", [128, NF, NT], F32, kind="ExternalInput").ap()
    yT_d = nc.dram_tensor("yT", [128, NF, CH * n_chunks], F32, kind="ExternalOutput").ap()
    win_d = nc.dram_tensor("win", [n_layers, 34, 128, 2048], F32, kind="ExternalInput").ap()
    wout_d = nc.dram_tensor("wout", [n_layers, 16, 128, 2048], F32, kind="ExternalInput").ap()
    wup_d = nc.dram_tensor("wup", [n_layers, 88, 128, 2048], F32, kind="ExternalInput").ap()
    wdn_d = nc.dram_tensor("wdn", [n_layers, 32, 128, WSLOT], F32, kind="ExternalInput").ap()
    pool_d = nc.dram_tensor("poolw", [128, n_layers * 4 * 128], F32, kind="ExternalInput").ap()
    pvec_d = nc.dram_tensor("pvec", [128, n_layers * NPL], F32, kind="ExternalInput").ap()
    cvec_d = nc.dram_tensor("cvec", [128, 65], F32, kind="ExternalInput").ap()

    with contextlib.ExitStack() as st:
        def sb(name, shape, dt):
            return st.enter_context(nc.sbuf_tensor(name, shape, dt))

        def mksem(name):
            return Sem(st.enter_context(nc.semaphore(name)))

        xT = sb("xT_sb", [128, NF * T], F32)
        oT = sb("oT_sb", [128, NF * T], F32)
        hT = oT.bitcast(BF16)
        big = sb("big_sb", [128, NFF * T], BF16)
        bigf = big.bitcast(F32)
        wring = [sb(f"wring{i}", [128, WSLOT], BF16) for i in range(NSLOT)]
        scr = [sb(f"scr{i}", [128, T], F32) for i in range(NSCR)]
        rsb = [sb(f"rsb{i}", [128, T], F32) for i in range(NRS)]
        scrb = [t_.bitcast(BF16) for t_ in scr]
        pvec = sb("pvec_sb", [128, n_layers * NPL], F32)
        poolw = sb("poolw_sb", [128, n_layers * 4 * 128], BF16)
        cvec = sb("cvec_sb", [128, 65], F32)
        ones = sb("ones_sb", [128, 128], F32)
        onesb = sb("onesb_sb", [128, 128], BF16)
        save_hm = sb("save_hm", [128, n_layers * NF * 30], BF16)
        save_hf = sb("save_hf", [128, n_layers * NF * 2], BF16)
        epsb = sb("eps_sb", [128, 2], F32)
        ps = st.enter_context(nc.psum_tensor("ps", [128, 8 * 512], F32))

        PE = Eng("pe", nc.tensor, mksem("s_pe"))
        ACT = Eng("act", nc.scalar, mksem("s_act"))
        DVE = Eng("dve", nc.vector, mksem("s_dve"))
        POOL = Eng("pool", nc.gpsimd, mksem("s_pool"))
        SP = Eng("sp", nc.sync, mksem("s_sp"))
        wsem = [mksem(f"s_w{i}") for i in range(NSLOT)]
        xsem = mksem("s_x")
        ysem = mksem("s_y")
        csem = [mksem(f"s_c{i}") for i in range(3)]

        block = st.enter_context(nc.Block())

        R_x = [Res() for _ in range(NF)]
        R_o = [Res() for _ in range(NF)]
        R_y = [Res() for _ in range(NF)]
        R_yb = [Res() for _ in range(6)]
        R_cv = [Res(), Res()]
        R_glu = [Res(), Res()]
        R_uc = Res()
        R_s = [Res(), Res()]
        R_act = [Res() for _ in range(NFF)]
        R_w = [Res() for _ in range(NSLOT)]
        R_scr = [Res() for _ in range(NSCR)]
        R_rs = [Res() for _ in range(NRS)]
        R_ps = [Res() for _ in range(8)]
        R_pv, R_pw, R_cvec, R_ones, R_eps = Res(), Res(), Res(), Res(), Res()
        R_shm = [Res() for _ in range(n_layers)]
        R_shf = [Res() for _ in range(n_layers)]

        def _deps(E, reads, writes, pe):
            deps = {}

            def add(s, v):
                if pe and s is E.sem:
                    return
                if deps.get(s, 0) < v:
                    deps[s] = v
            for r in reads:
                if r.w is not None:
                    add(*r.w)
            for w in writes:
                if w.w is not None:
                    add(*w.w)
                for s, v in w.r.items():
                    add(s, v)
            for s, v in deps.items():
                if E.seen.get(s, 0) < v:
                    E.e.wait_ge(s.h, v)
                    E.seen[s] = v

        def _commit(tok, reads, writes):
            s, v = tok
            for r in reads:
                if r.r.get(s, 0) < v:
                    r.r[s] = v
            for w in writes:
                w.w = tok
                w.r = {}

        def op(E, fn, reads=(), writes=(), pe=False):
            _deps(E, reads, writes, pe)
            ins = fn()
            E.sem.cnt += 1
            ins.then_inc(E.sem.h, 1)
            tok = (E.sem, E.sem.cnt)
            _commit(tok, reads, writes)
            return tok

        def dma(E, fn, sem, reads=(), writes=()):
            _deps(E, reads, writes, False)
            ins = fn()
            sem.cnt += 16
            ins.then_inc(sem.h, 16)
            tok = (sem, sem.cnt)
            _commit(tok, reads, writes)
            return tok

        state = {"bank": 7, "scr": 0, "rs": 0, "slot": 0, "pinned": set()}

        def alloc_bank():
            while True:
                state["bank"] = (state["bank"] + 1) % 8
                if state["bank"] not in state["pinned"]:
                    return state["bank"]

        def nscr():
            state["scr"] = (state["scr"] + 1) % NSCR
            return state["scr"]

        def nrs():
            state["rs"] = (state["rs"] + 1) % NRS
            return state["rs"]

        def PS(b, c0, c1):
            return ps[:, b * 512 + c0: b * 512 + c1]

        def X(ft, t0, t1):
            return xT[:, ft * T + t0: ft * T + t1]

        def H(ft, t0, t1):
            return hT[:, ft * T + t0: ft * T + t1]

        def O(m, t0, t1):
            return oT[:, m * T + t0: m * T + t1]

        def Y(k, t0, t1):
            return big[:, k * T + t0: k * T + t1]

        def A(j, t0, t1):
            return big[:, j * T + t0: j * T + t1]

        YB0 = 8 * T
        PB = PADL + T
        CV0 = 14 * T

        def YB(j, t0, t1):
            return bigf[:, YB0 + j * T + t0: YB0 + j * T + t1]

        def padbuf(idx):
            base = CV0 + idx * PB + PADL

            def f(t0, t1):
                return bigf[:, base + t0: base + t1]
            return f
        CV = [padbuf(0), padbuf(1)]
        GLU = [padbuf(2), padbuf(3)]
        UC = padbuf(4)
        SS = [padbuf(5), padbuf(6)]

        def pv(l, off, n=1):
            return pvec[:, l * NPL + off: l * NPL + off + n]

        act = nc.scalar.activation
        V = nc.vector

        def load_w(src_ap, nelem):
            s = state["slot"]
            state["slot"] = (s + 1) % NSLOT
            dma(POOL, lambda: nc.gpsimd.dma_start(out=wring[s][:, 0:nelem], in_=src_ap), wsem[s],
                writes=[R_w[s]])
            return s

        def mm_group(bank, n, pairs, reads):
            def fn():
                ins = None
                last = len(pairs) - 1
                for i, (l_, r_) in enumerate(pairs):
                    ins = nc.tensor.matmul(PS(bank, 0, n), l_, r_, start=(i == 0), stop=(i == last))
                return ins
            return op(PE, fn, reads=reads, writes=[R_ps[bank]], pe=True)

        def ones_mm(bank, n, rhs_ap, rhs_res, first, last):
            return op(PE, lambda: nc.tensor.matmul(PS(bank, 0, n), ones[:, :], rhs_ap, start=first, stop=last),
                      reads=[rhs_res, R_ones], writes=[R_ps[bank]], pe=True)

        def ones_mm_bf(bank, n, rhs_ap, rhs_res, first, last):
            return op(PE, lambda: nc.tensor.matmul(PS(bank, 0, n), onesb[:, :], rhs_ap, start=first, stop=last),
                      reads=[rhs_res, R_ones], writes=[R_ps[bank]], pe=True)

        def proj(slot, rhs_fn, rhs_res, w0, t1):
            bank = alloc_bank()
            pairs = [(wring[slot][:, kt * 128:(kt + 1) * 128], rhs_fn(kt, w0, t1)) for kt in range(16)]
            mm_group(bank, t1 - w0, pairs, [R_w[slot]] + rhs_res)
            return bank

        def finish_rstd(banks, tl, r0, dim, epscol, chunk, use_mask):
            rs = nrs()
            for ti, (t0, t1) in enumerate(tl):
                n = t1 - t0
                op(ACT, lambda: act(out=rsb[rs][:, t0:t1], in_=PS(banks[ti], 0, n), func=AF.Sqrt,
                                    bias=epsb[:, epscol:epscol + 1], scale=1.0 / dim),
                   reads=[R_ps[banks[ti]], R_eps], writes=[R_rs[rs]])
            op(DVE, lambda: V.reciprocal(out=rsb[rs][:, r0:T], in_=rsb[rs][:, r0:T]),
               reads=[R_rs[rs]], writes=[R_rs[rs]])
            if use_mask and chunk == 0 and r0 < HALO:
                op(DVE, lambda: V.tensor_scalar(out=rsb[rs][:, r0:HALO], in0=rsb[rs][:, r0:HALO],
                                                scalar1=cvec[:, 0:1], scalar2=None, op0=ALU.mult),
                   reads=[R_rs[rs], R_cvec], writes=[R_rs[rs]])
            for b in banks:
                state["pinned"].discard(b)
            return rs

        def pre_norm(l, r0, goff):
            tl = tiles_of(r0)
            banks = [alloc_bank(), alloc_bank()]
            state["pinned"].update(banks)
            for ft in range(NF):
                sc = nscr()
                op(ACT, lambda: act(out=scrb[sc][:, r0:T], in_=X(ft, r0, T), func=AF.Square),
                   reads=[R_x[ft]], writes=[R_scr[sc]])
                for ti, (t0, t1) in enumerate(tl):
                    ones_mm_bf(banks[ti], t1 - t0, scrb[sc][:, t0:t1], R_scr[sc], ft == 0, ft == NF - 1)
            rs = finish_rstd(banks, tl, r0, float(D), 0, None, False)
            for ft in range(NF):
                op(DVE, lambda: V.scalar_tensor_tensor(out=H(ft, r0, T), in0=X(ft, r0, T),
                                                       scalar=pv(l, goff + ft), in1=rsb[rs][:, r0:T],
                                                       op0=ALU.mult, op1=ALU.mult),
                   reads=[R_x[ft], R_rs[rs], R_pv], writes=[R_o[ft // 2]])

        def post_norm_residual(l, r0, goff, banks, tl, chunk):
            rs = finish_rstd(banks, tl, r0, float(D), 0, chunk, True)
            for m in range(NF):
                op(DVE, lambda: V.scalar_tensor_tensor(out=O(m, r0, T), in0=O(m, r0, T),
                                                       scalar=pv(l, goff + m), in1=rsb[rs][:, r0:T],
                                                       op0=ALU.mult, op1=ALU.mult),
                   reads=[R_o[m], R_rs[rs], R_pv], writes=[R_o[m]])
                op(DVE, lambda: V.tensor_tensor(out=X(m, r0, T), in0=X(m, r0, T), in1=O(m, r0, T), op=ALU.add),
                   reads=[R_x[m], R_o[m]], writes=[R_x[m]])

        def evac_with_sq(bank, n, m, t0, t1, ssbank, first, last):
            op(DVE, lambda: V.tensor_copy(out=O(m, t0, t1), in_=PS(bank, 0, n)),
               reads=[R_ps[bank]], writes=[R_o[m]])

        def o_stats(r0, banks, tl):
            for m in range(NF):
                sc = nscr()
                op(ACT, lambda: act(out=scrb[sc][:, r0:T], in_=O(m, r0, T), func=AF.Square),
                   reads=[R_o[m]], writes=[R_scr[sc]])
                for ti, (t0, t1) in enumerate(tl):
                    ones_mm_bf(banks[ti], t1 - t0, scrb[sc][:, t0:t1], R_scr[sc], m == 0, m == NF - 1)

        def layer(l, chunk):
            first = (chunk == 0)
            last_chunk = (chunk == n_chunks - 1)
            if first:
                a = 32 * l
                b = 32 * l + 28
                hm0, hf0 = a, b - 2
            else:
                a = HALO - 30
                b = HALO
                hm0, hf0 = HALO, HALO
            tla = tiles_of(a)
            tlb = tiles_of(b)
            tlo = [(b, T)] if (T - b) <= 512 else tlb
            hres = [R_o[i] for i in range(8)]
            hT3 = hT[:, 0:NF * T].rearrange("p (f t) -> p f t", t=T)
            shm3 = save_hm[:, l * NF * 30:(l + 1) * NF * 30].rearrange("p (f c) -> p f c", c=30)
            shf3 = save_hf[:, l * NF * 2:(l + 1) * NF * 2].rearrange("p (f c) -> p f c", c=2)

            pre_norm(l, hm0, OFF_GMP)
            if not first:
                op(DVE, lambda: V.tensor_copy(out=hT3[:, :, HALO - 30:HALO], in_=shm3),
                   reads=[R_shm[l]], writes=hres)
            if not last_chunk:
                op(DVE, lambda: V.tensor_copy(out=shm3, in_=hT3[:, :, T - 30:T]),
                   reads=hres, writes=[R_shm[l]])

            def b_proj(j):
                sv = load_w(win_d[l, 18 + j], 2048)
                sg = load_w(win_d[l, 24 + j], 2048)
                gi = j % 2
                for (t0, t1) in tla:
                    n = t1 - t0
                    bv = proj(sv, H, hres, t0, t1)
                    bg = proj(sg, H, hres, t0, t1)
                    sc = nscr()
                    op(ACT, lambda: act(out=scr[sc][:, 0:n], in_=PS(bg, 0, n), func=AF.Sigmoid),
                       reads=[R_ps[bg]], writes=[R_scr[sc]])
                    op(DVE, lambda: V.tensor_tensor(out=GLU[gi](t0, t1), in0=scr[sc][:, 0:n], in1=PS(bv, 0, n),
                                                    op=ALU.mult),
                       reads=[R_scr[sc], R_ps[bv]], writes=[R_glu[gi]])

            def b_conv(j):
                gi = j % 2
                op(ACT, lambda: act(out=YB(j, b, T), in_=GLU[gi](b - 30, T - 30), func=AF.Identity,
                                    bias=pv(l, OFF_CBB + j), scale=pv(l, OFF_CB + j * 31)),
                   reads=[R_glu[gi], R_pv], writes=[R_yb[j]])
                for k in range(1, 31):
                    op(DVE, lambda: V.scalar_tensor_tensor(out=YB(j, b, T), in0=GLU[gi](b - 30 + k, T - 30 + k),
                                                           scalar=pv(l, OFF_CB + j * 31 + k), in1=YB(j, b, T),
                                                           op0=ALU.mult, op1=ALU.add),
                       reads=[R_glu[gi], R_pv, R_yb[j]], writes=[R_yb[j]])

            def a_head(j):
                sb_ = load_w(win_d[l, j], 2048)
                sc_ = load_w(win_d[l, 6 + j], 2048)
                sv_ = load_w(win_d[l, 12 + j], 2048)
                ci = j % 2
                for (t0, t1) in tla:
                    n = t1 - t0
                    bb = proj(sb_, H, hres, t0, t1)
                    bc = proj(sc_, H, hres, t0, t1)
                    bv = proj(sv_, H, hres, t0, t1)
                    s1 = nscr()
                    op(ACT, lambda: act(out=scr[s1][:, 0:n], in_=PS(bc, 0, n), func=AF.Copy),
                       reads=[R_ps[bc]], writes=[R_scr[s1]])
                    op(DVE, lambda: V.tensor_tensor(out=CV[ci](t0, t1), in0=scr[s1][:, 0:n], in1=PS(bv, 0, n),
                                                    op=ALU.mult),
                       reads=[R_scr[s1], R_ps[bv]], writes=[R_cv[ci]])
                    s2 = nscr()
                    op(ACT, lambda: act(out=scr[s2][:, 0:n], in_=CV[ci](t0 - 2, t1 - 2), func=AF.Identity,
                                        scale=pv(l, OFF_CA + j * 3)),
                       reads=[R_cv[ci], R_pv], writes=[R_scr[s2]])
                    for k in (1, 2):
                        op(DVE, lambda: V.scalar_tensor_tensor(out=scr[s2][:, 0:n], in0=CV[ci](t0 - 2 + k, t1 - 2 + k),
                                                               scalar=pv(l, OFF_CA + j * 3 + k), in1=scr[s2][:, 0:n],
                                                               op0=ALU.mult, op1=ALU.add),
                           reads=[R_cv[ci], R_pv, R_scr[s2]], writes=[R_scr[s2]])
                    op(DVE, lambda: V.tensor_tensor(out=Y(j, t0, t1), in0=scr[s2][:, 0:n], in1=PS(bb, 0, n),
                                                    op=ALU.mult),
                       reads=[R_scr[s2], R_ps[bb]], writes=[R_y[j]])


            for j in range(6):
                b_proj(j)
                a_head(j)
                b_conv(j)

            for g in range(4):
                su = load_w(win_d[l, 30 + g], 2048)
                w = POOL_W[g]
                for (t0, t1) in tla:
                    n = t1 - t0
                    bu = proj(su, H, hres, t0, t1)
                    op(ACT, lambda: act(out=UC(t0, t1), in_=PS(bu, 0, n), func=AF.Copy),
                       reads=[R_ps[bu]], writes=[R_uc])
                src, src_res = UC, R_uc
                for k in range(1, g + 2):
                    sh = 1 << (k - 1)
                    lo = a - (w - (1 << k))
                    di = (k - 1) % 2
                    dst, dst_res = SS[di], R_s[di]
                    op(DVE, lambda: V.tensor_tensor(out=dst(lo, T), in0=src(lo, T), in1=src(lo - sh, T - sh),
                                                    op=ALU.add),
                       reads=[src_res], writes=[dst_res])
                    src, src_res = dst, dst_res
                op(DVE, lambda: V.scalar_tensor_tensor(out=Y(12 + g, a, T), in0=src(a, T), scalar=1.0 / w,
                                                       in1=UC(a, T), op0=ALU.mult, op1=ALU.subtract),
                   reads=[src_res, R_uc], writes=[R_y[12 + g]])
                if chunk == 0:
                    sc = nscr()
                    op(DVE, lambda: V.tensor_tensor(out=scr[sc][:, 0:16], in0=src(HALO, HALO + 16),
                                                    in1=cvec[:, 1 + g * 16: 17 + g * 16], op=ALU.mult),
                       reads=[src_res, R_cvec], writes=[R_scr[sc]])
                    op(DVE, lambda: V.tensor_tensor(out=Y(12 + g, HALO, HALO + 16), in0=scr[sc][:, 0:16],
                                                    in1=UC(HALO, HALO + 16), op=ALU.subtract),
                       reads=[R_scr[sc], R_uc], writes=[R_y[12 + g]])
                for (t0, t1) in tla:
                    n = t1 - t0
                    bm = alloc_bank()
                    pw_ap = poolw[:, (l * 4 + g) * 128:(l * 4 + g + 1) * 128]
                    op(PE, lambda: nc.tensor.matmul(PS(bm, 0, n), pw_ap, Y(12 + g, t0, t1), start=True, stop=True),
                       reads=[R_pw, R_y[12 + g]], writes=[R_ps[bm]], pe=True)
                    op(DVE, lambda: V.tensor_scalar(out=Y(12 + g, t0, t1), in0=PS(bm, 0, n),
                                                    scalar1=pv(l, OFF_PS + g), scalar2=None, op0=ALU.mult),
                       reads=[R_ps[bm], R_pv], writes=[R_y[12 + g]])

            b1 = [alloc_bank(), alloc_bank()]
            b2 = [alloc_bank(), alloc_bank()]
            state["pinned"].update(b1 + b2)
            for j in range(6):
                for ti, (t0, t1) in enumerate(tlb):
                    ones_mm(b1[ti], t1 - t0, YB(j, t0, t1), R_yb[j], j == 0, j == 5)
            for j in range(6):
                sc = nscr()
                op(ACT, lambda: act(out=scrb[sc][:, b:T], in_=YB(j, b, T), func=AF.Square),
                   reads=[R_yb[j]], writes=[R_scr[sc]])
                for ti, (t0, t1) in enumerate(tlb):
                    ones_mm_bf(b2[ti], t1 - t0, scrb[sc][:, t0:t1], R_scr[sc], j == 0, j == 5)
            rmean, rvar, rnmr = nrs(), nrs(), nrs()
            for ti, (t0, t1) in enumerate(tlb):
                n = t1 - t0
                op(DVE, lambda: V.tensor_scalar(out=rsb[rmean][:, t0:t1], in0=PS(b1[ti], 0, n),
                                                scalar1=1.0 / 768.0, scalar2=None, op0=ALU.mult),
                   reads=[R_ps[b1[ti]]], writes=[R_rs[rmean]])
            sc = nscr()
            op(DVE, lambda: V.tensor_tensor(out=scr[sc][:, b:T], in0=rsb[rmean][:, b:T], in1=rsb[rmean][:, b:T],
                                            op=ALU.mult),
               reads=[R_rs[rmean]], writes=[R_scr[sc]])
            for ti, (t0, t1) in enumerate(tlb):
                n = t1 - t0
                op(DVE, lambda: V.scalar_tensor_tensor(out=rsb[rvar][:, t0:t1], in0=PS(b2[ti], 0, n),
                                                       scalar=1.0 / 768.0, in1=scr[sc][:, t0:t1],
                                                       op0=ALU.mult, op1=ALU.subtract),
                   reads=[R_ps[b2[ti]], R_scr[sc]], writes=[R_rs[rvar]])
            for bb in b1 + b2:
                state["pinned"].discard(bb)
            op(ACT, lambda: act(out=rsb[rvar][:, b:T], in_=rsb[rvar][:, b:T], func=AF.Sqrt,
                                bias=epsb[:, 1:2], scale=1.0),
               reads=[R_rs[rvar], R_eps], writes=[R_rs[rvar]])
            op(DVE, lambda: V.reciprocal(out=rsb[rvar][:, b:T], in_=rsb[rvar][:, b:T]),
               reads=[R_rs[rvar]], writes=[R_rs[rvar]])
            op(DVE, lambda: V.scalar_tensor_tensor(out=rsb[rnmr][:, b:T], in0=rsb[rmean][:, b:T], scalar=-1.0,
                                                   in1=rsb[rvar][:, b:T], op0=ALU.mult, op1=ALU.mult),
               reads=[R_rs[rmean], R_rs[rvar]], writes=[R_rs[rnmr]])
            for j in range(6):
                sc = nscr()
                op(DVE, lambda: V.tensor_tensor(out=scr[sc][:, b:T], in0=YB(j, b, T), in1=rsb[rvar][:, b:T],
                                                op=ALU.mult),
                   reads=[R_yb[j], R_rs[rvar]], writes=[R_scr[sc]])
                op(DVE, lambda: V.tensor_tensor(out=scr[sc][:, b:T], in0=scr[sc][:, b:T], in1=rsb[rnmr][:, b:T],
                                                op=ALU.add),
                   reads=[R_scr[sc], R_rs[rnmr]], writes=[R_scr[sc]])
                op(ACT, lambda: act(out=Y(6 + j, b, T), in_=scr[sc][:, b:T], func=AF.Silu,
                                    bias=pv(l, OFF_LNB + j), scale=pv(l, OFF_LNG + j)),
                   reads=[R_scr[sc], R_pv], writes=[R_y[6 + j]])

            bss = [alloc_bank() for _ in tlo]
            state["pinned"].update(bss)
            for m in range(NF):
                s = load_w(wout_d[l, m], 2048)
                for ti, (t0, t1) in enumerate(tlo):
                    bank = proj(s, Y, R_y, t0, t1)
                    evac_with_sq(bank, t1 - t0, m, t0, t1, bss[ti], m == 0, m == NF - 1)
            o_stats(b, bss, tlo)
            post_norm_residual(l, b, OFF_GMO, bss, tlo, chunk)

            pre_norm(l, hf0, OFF_GFP)
            if not first:
                op(DVE, lambda: V.tensor_copy(out=hT3[:, :, HALO - 2:HALO], in_=shf3),
                   reads=[R_shf[l]], writes=hres)
            if not last_chunk:
                op(DVE, lambda: V.tensor_copy(out=shf3, in_=hT3[:, :, T - 2:T]),
                   reads=hres, writes=[R_shf[l]])
            for j in range(NFF):
                sg = load_w(wup_d[l, j], 2048)
                sv = load_w(wup_d[l, NFF + j], 2048)
                for (t0, t1) in tlb:
                    n = t1 - t0
                    w0 = t0 - 2
                    bg = proj(sg, H, hres, w0, t1)
                    bv = proj(sv, H, hres, w0, t1)
                    accs = []
                    for (bank, m) in ((bg, j), (bv, NFF + j)):
                        sc = nscr()
                        op(DVE, lambda: V.tensor_scalar(out=scr[sc][:, 0:n], in0=PS(bank, 2, n + 2),
                                                        scalar1=pv(l, OFF_CF + m * 3 + 2), scalar2=pv(l, OFF_CFB + m),
                                                        op0=ALU.mult, op1=ALU.add),
                           reads=[R_ps[bank], R_pv], writes=[R_scr[sc]])
                        for k in (1, 0):
                            op(DVE, lambda: V.scalar_tensor_tensor(out=scr[sc][:, 0:n], in0=PS(bank, k, n + k),
                                                                   scalar=pv(l, OFF_CF + m * 3 + k),
                                                                   in1=scr[sc][:, 0:n], op0=ALU.mult, op1=ALU.add),
                               reads=[R_ps[bank], R_pv, R_scr[sc]], writes=[R_scr[sc]])
                        accs.append(sc)
                    sg_, sv_ = accs
                    op(ACT, lambda: act(out=scr[sg_][:, 0:n], in_=scr[sg_][:, 0:n], func=AF.Silu),
                       reads=[R_scr[sg_]], writes=[R_scr[sg_]])
                    op(DVE, lambda: V.tensor_tensor(out=A(j, t0, t1), in0=scr[sg_][:, 0:n], in1=scr[sv_][:, 0:n],
                                                    op=ALU.mult),
                       reads=[R_scr[sg_], R_scr[sv_]], writes=[R_act[j]])

            bss = [alloc_bank() for _ in tlo]
            state["pinned"].update(bss)
            for m in range(NF):
                s0 = load_w(wdn_d[l, 2 * m], WSLOT)
                s1 = load_w(wdn_d[l, 2 * m + 1], WSLOT)
                for ti, (t0, t1) in enumerate(tlo):
                    bank = alloc_bank()
                    pairs = []
                    for kt in range(NFF):
                        ws = wring[s0] if kt < 22 else wring[s1]
                        kk = kt % 22
                        pairs.append((ws[:, kk * 128:(kk + 1) * 128], A(kt, t0, t1)))
                    mm_group(bank, t1 - t0, pairs, [R_w[s0], R_w[s1]] + R_act)
                    evac_with_sq(bank, t1 - t0, m, t0, t1, bss[ti], m == 0, m == NF - 1)
            o_stats(b, bss, tlo)
            post_norm_residual(l, b, OFF_GFO, bss, tlo, chunk)

        dma(SP, lambda: nc.sync.dma_start(out=pvec[:, :], in_=pvec_d), csem[0], writes=[R_pv])
        dma(SP, lambda: nc.sync.dma_start(out=cvec[:, :], in_=cvec_d), csem[1], writes=[R_cvec])
        dma(POOL, lambda: nc.gpsimd.dma_start(out=poolw[:, :], in_=pool_d), csem[2], writes=[R_pw])
        op(DVE, lambda: V.memset(ones[:, :], 1.0), writes=[R_ones])
        op(DVE, lambda: V.memset(onesb[:, :], 1.0), writes=[R_ones])
        op(DVE, lambda: V.memset(epsb[:, 0:1], 1e-6), writes=[R_eps])
        op(DVE, lambda: V.memset(epsb[:, 1:2], 1e-5), writes=[R_eps])
        op(DVE, lambda: V.memset(oT[:, :], 0.0), writes=R_o)
        op(DVE, lambda: V.memset(bigf[:, :], 0.0),
           writes=R_y + R_yb + R_cv + R_glu + [R_uc] + R_s + R_act)
        for i in range(NSCR):
            op(DVE, lambda: V.memset(scr[i][:, :], 0.0), writes=[R_scr[i]])
        for i in range(NRS):
            op(DVE, lambda: V.memset(rsb[i][:, :], 0.0), writes=[R_rs[i]])

        xT3 = xT[:, :].rearrange("p (f t) -> p f t", t=T)
        for c in range(n_chunks):
            dma(SP, lambda: nc.sync.dma_start(out=xT3, in_=xT_d[:, :, c * CH: c * CH + T]), xsem, writes=R_x)
            for l in range(n_layers):
                layer(l, c)
            dma(SP, lambda: nc.sync.dma_start(out=yT_d[:, :, c * CH:(c + 1) * CH], in_=xT3[:, :, HALO:T]),
                ysem, reads=R_x)
        nc.sync.wait_ge(ysem.h, ysem.cnt)
    return nc


def _fm(vec):
    return np.ascontiguousarray(vec.reshape(-1, 128).T)


def _pack_layer_params(p, l):
    cols = [
        _fm(p["norm_mix_pre"][l]), _fm(p["norm_mix_post"][l]),
        _fm(p["norm_ffn_pre"][l]), _fm(p["norm_ffn_post"][l]),
        p["conv_a_w"][l].reshape(3, 6, 128).transpose(2, 1, 0).reshape(128, 18),
        p["conv_b_w"][l].reshape(31, 6, 128).transpose(2, 1, 0).reshape(128, 186),
        _fm(p["conv_b_bias"][l]), _fm(p["ln_b_gain"][l]), _fm(p["ln_b_bias"][l]),
        _fm(p["pool_scale"][l]),
        p["conv_ffn_w"][l].reshape(3, 88, 128).transpose(2, 1, 0).reshape(128, 264),
        _fm(p["conv_ffn_bias"][l]),
    ]
    out = np.concatenate(cols, axis=1).astype(np.float32)
    assert out.shape == (128, NPL)
    return out


def _tile_w(w, n_m):
    return np.ascontiguousarray(w.reshape(16, 128, n_m, 128).transpose(2, 1, 0, 3).reshape(n_m, 128, 2048))


def _tile_wdn(w):
    return np.ascontiguousarray(
        w.reshape(2, 22, 128, 16, 128).transpose(3, 0, 2, 1, 4).reshape(32, 128, WSLOT))


def _layer_weights(p, layers):
    win = np.stack([_tile_w(p["w_in"][l], 34) for l in layers])
    wout = np.stack([_tile_w(p["w_out"][l], 16) for l in layers])
    wup = np.stack([_tile_w(p["w_up"][l], 88) for l in layers])
    wdn = np.stack([_tile_wdn(p["w_down"][l]) for l in layers])
    poolw = np.concatenate([p["pool_w"][l].transpose(1, 0, 2).reshape(128, 4 * 128) for l in layers], axis=1)
    pvec = np.concatenate([_pack_layer_params(p, l) for l in layers], axis=1)
    return dict(win=win, wout=wout, wup=wup, wdn=wdn,
                poolw=np.ascontiguousarray(poolw.astype(np.float32)),
                pvec=np.ascontiguousarray(pvec))


def _core_consts(core):
    cv = np.zeros((128, 65), np.float32)
    start = (core * TOK_PER_CORE) % SEQ == 0
    cv[:, 0] = 0.0 if start else 1.0
    for g, w in enumerate(POOL_W):
        for i in range(16):
            cv[:, 1 + g * 16 + i] = 1.0 / (min(i + 1, w) if start else w)
    return cv


def _shard_x(xflat, n_chunks_total=4):
    outs = []
    for core in range(N_CORES):
        s = core * TOK_PER_CORE
        blk = np.zeros((HALO + TOK_PER_CORE, D), np.float32)
        if s % SEQ == 0:
            blk[HALO:] = xflat[s:s + TOK_PER_CORE]
        else:
            blk[:] = xflat[s - HALO:s + TOK_PER_CORE]
        outs.append(np.ascontiguousarray(blk.T.reshape(NF, 128, -1).transpose(1, 0, 2)))
    return outs


def _unshard_y(res):
    outs = []
    for r in res:
        yT = np.asarray(r["yT"])
        outs.append(yT.transpose(2, 1, 0).reshape(TOK_PER_CORE, D))
    return np.concatenate(outs, axis=0)


_NC_CACHE = {}


def _get_nc(n_layers, n_chunks):
    key = (n_layers, n_chunks)
    if key not in _NC_CACHE:
        _NC_CACHE[key] = build(n_layers, n_chunks)
    return _NC_CACHE[key]


FUSED = True


def kernel(**inputs):
    p = {k: np.asarray(v, dtype=np.float32) for k, v in inputs.items()}
    x = p["x"]
    B, S, _ = x.shape
    xflat = x.reshape(B * S, D)
    consts = [_core_consts(c) for c in range(N_CORES)]
    if FUSED:
        nc = _get_nc(DEPTH, 4)
        w = _layer_weights(p, range(DEPTH))
        xs = _shard_x(xflat)
        in_maps = [dict(xT=xs[c], cvec=consts[c], **w) for c in range(N_CORES)]
        res = run_bass_kernel_spmd(nc, in_maps, core_ids=list(range(N_CORES)))
        xflat = _unshard_y(res.results)
    else:
        nc = _get_nc(1, 4)
        for l in range(DEPTH):
            w = _layer_weights(p, [l])
            xs = _shard_x(xflat)
            in_maps = [dict(xT=xs[c], cvec=consts[c], **w) for c in range(N_CORES)]
            res = run_bass_kernel_spmd(nc, in_maps, core_ids=list(range(N_CORES)))
            xflat = _unshard_y(res.results)
    return xflat.reshape(B, S, D).astype(np.float32)
```

```python
import contextlib
import numpy as np
import concourse.bass as bass
import concourse.mybir as mybir
from concourse.bass_utils import run_bass_kernel_spmd

F32 = mybir.dt.float32
BF16 = mybir.dt.bfloat16
AF = mybir.ActivationFunctionType
ALU = mybir.AluOpType

N_CORES = 8
D = 2048
NF = 16
NFF = 44
DEPTH = 4
SEQ = 8192
TOK_PER_CORE = 2048
T = 640
PADL = 32
CH = 512
HALO = 128
WSLOT = 2816
NSLOT = 6
NSCR = 4
NRS = 4

NPL = 642
OFF_GMP, OFF_GMO, OFF_GFP, OFF_GFO = 0, 16, 32, 48
OFF_CA, OFF_CB, OFF_CBB, OFF_LNG, OFF_LNB, OFF_PS, OFF_CF, OFF_CFB = 64, 82, 268, 274, 280, 286, 290, 554
POOL_W = (2, 4, 8, 16)


class Sem:
    def __init__(self, h):
        self.h = h
        self.cnt = 0


class Eng:
    def __init__(self, name, e, sem):
        self.name = name
        self.e = e
        self.sem = sem
        self.seen = {}


class Res:
    __slots__ = ("w", "r")

    def __init__(self):
        self.w = None
        self.r = {}


def tiles_of(r0):
    n = T - r0
    h = ((n + 1) // 2 + 1) // 2 * 2
    return [(r0, r0 + h), (r0 + h, T)]


def build(n_layers, n_chunks, n_ranks=N_CORES):
    nc = bass.Bass("TRN2", target_bir_lowering=False)
    NT = HALO + CH * n_chunks
    xT_d = nc.dram_tensor("xT", [128, NF, NT], F32, kind="ExternalInput").ap()
    yT_d = nc.dram_tensor("yT", [128, NF, CH * n_chunks], F32, kind="ExternalOutput").ap()
    win_d = nc.dram_tensor("win", [n_layers, 34, 128, 2048], F32, kind="ExternalInput").ap()
    wout_d = nc.dram_tensor("wout", [n_layers, 16, 128, 2048], F32, kind="ExternalInput").ap()
    wup_d = nc.dram_tensor("wup", [n_layers, 88, 128, 2048], F32, kind="ExternalInput").ap()
    wdn_d = nc.dram_tensor("wdn", [n_layers, 32, 128, WSLOT], F32, kind="ExternalInput").ap()
    pool_d = nc.dram_tensor("poolw", [128, n_layers * 4 * 128], F32, kind="ExternalInput").ap()
    pvec_d = nc.dram_tensor("pvec", [128, n_layers * NPL], F32, kind="ExternalInput").ap()
    cvec_d = nc.dram_tensor("cvec", [128, 65], F32, kind="ExternalInput").ap()

    with contextlib.ExitStack() as st:
        def sb(name, shape, dt):
            return st.enter_context(nc.sbuf_tensor(name, shape, dt))

        def mksem(name):
            return Sem(st.enter_context(nc.semaphore(name)))

        xT = sb("xT_sb", [128, NF * T], F32)
        oT = sb("oT_sb", [128, NF * T], F32)
        hT = oT.bitcast(BF16)
        big = sb("big_sb", [128, NFF * T], BF16)
        bigf = big.bitcast(F32)
        wring = [sb(f"wring{i}", [128, WSLOT], BF16) for i in range(NSLOT)]
        scr = [sb(f"scr{i}", [128, T], F32) for i in range(NSCR)]
        rsb = [sb(f"rsb{i}", [128, T], F32) for i in range(NRS)]
        scrb = [t_.bitcast(BF16) for t_ in scr]
        pvec = sb("pvec_sb", [128, n_layers * NPL], F32)
        poolw = sb("poolw_sb", [128, n_layers * 4 * 128], BF16)
        cvec = sb("cvec_sb", [128, 65], F32)
        ones = sb("ones_sb", [128, 128], F32)
        onesb = sb("onesb_sb", [128, 128], BF16)
        save_hm = sb("save_hm", [128, n_layers * NF * 30], BF16)
        save_hf = sb("save_hf", [128, n_layers * NF * 2], BF16)
        epsb = sb("eps_sb", [128, 2], F32)
        ps = st.enter_context(nc.psum_tensor("ps", [128, 8 * 512], F32))

        PE = Eng("pe", nc.tensor, mksem("s_pe"))
        ACT = Eng("act", nc.scalar, mksem("s_act"))
        DVE = Eng("dve", nc.vector, mksem("s_dve"))
        POOL = Eng("pool", nc.gpsimd, mksem("s_pool"))
        SP = Eng("sp", nc.sync, mksem("s_sp"))
        wsem = [mksem(f"s_w{i}") for i in range(NSLOT)]
        xsem = mksem("s_x")
        ysem = mksem("s_y")
        csem = [mksem(f"s_c{i}") for i in range(3)]

        block = st.enter_context(nc.Block())

        R_x = [Res() for _ in range(NF)]
        R_o = [Res() for _ in range(NF)]
        R_y = [Res() for _ in range(NF)]
        R_yb = [Res() for _ in range(6)]
        R_cv = [Res(), Res()]
        R_glu = [Res(), Res()]
        R_uc = Res()
        R_s = [Res(), Res()]
        R_act = [Res() for _ in range(NFF)]
        R_w = [Res() for _ in range(NSLOT)]
        R_scr = [Res() for _ in range(NSCR)]
        R_rs = [Res() for _ in range(NRS)]
        R_ps = [Res() for _ in range(8)]
        R_pv, R_pw, R_cvec, R_ones, R_eps = Res(), Res(), Res(), Res(), Res()
        R_shm = [Res() for _ in range(n_layers)]
        R_shf = [Res() for _ in range(n_layers)]

        def _deps(E, reads, writes, pe):
            deps = {}

            def add(s, v):
                if pe and s is E.sem:
                    return
                if deps.get(s, 0) < v:
                    deps[s] = v
            for r in reads:
                if r.w is not None:
                    add(*r.w)
            for w in writes:
                if w.w is not None:
                    add(*w.w)
                for s, v in w.r.items():
                    add(s, v)
            for s, v in deps.items():
                if E.seen.get(s, 0) < v:
                    E.e.wait_ge(s.h, v)
                    E.seen[s] = v

        def _commit(tok, reads, writes):
            s, v = tok
            for r in reads:
                if r.r.get(s, 0) < v:
                    r.r[s] = v
            for w in writes:
                w.w = tok
                w.r = {}

        def op(E, fn, reads=(), writes=(), pe=False):
            _deps(E, reads, writes, pe)
            ins = fn()
            E.sem.cnt += 1
            ins.then_inc(E.sem.h, 1)
            tok = (E.sem, E.sem.cnt)
            _commit(tok, reads, writes)
            return tok

        def dma(E, fn, sem, reads=(), writes=()):
            _deps(E, reads, writes, False)
            ins = fn()
            sem.cnt += 16
            ins.then_inc(sem.h, 16)
            tok = (sem, sem.cnt)
            _commit(tok, reads, writes)
            return tok

        state = {"bank": 7, "scr": 0, "rs": 0, "slot": 0, "pinned": set()}

        def alloc_bank():
            while True:
                state["bank"] = (state["bank"] + 1) % 8
                if state["bank"] not in state["pinned"]:
                    return state["bank"]

        def nscr():
            state["scr"] = (state["scr"] + 1) % NSCR
            return state["scr"]

        def nrs():
            state["rs"] = (state["rs"] + 1) % NRS
            return state["rs"]

        def PS(b, c0, c1):
            return ps[:, b * 512 + c0: b * 512 + c1]

        def X(ft, t0, t1):
            return xT[:, ft * T + t0: ft * T + t1]

        def H(ft, t0, t1):
            return hT[:, ft * T + t0: ft * T + t1]

        def O(m, t0, t1):
            return oT[:, m * T + t0: m * T + t1]

        def Y(k, t0, t1):
            return big[:, k * T + t0: k * T + t1]

        def A(j, t0, t1):
            return big[:, j * T + t0: j * T + t1]

        YB0 = 8 * T
        PB = PADL + T
        CV0 = 14 * T

        def YB(j, t0, t1):
            return bigf[:, YB0 + j * T + t0: YB0 + j * T + t1]

        def padbuf(idx):
            base = CV0 + idx * PB + PADL

            def f(t0, t1):
                return bigf[:, base + t0: base + t1]
            return f
        CV = [padbuf(0), padbuf(1)]
        GLU = [padbuf(2), padbuf(3)]
        UC = padbuf(4)
        SS = [padbuf(5), padbuf(6)]

        def pv(l, off, n=1):
            return pvec[:, l * NPL + off: l * NPL + off + n]

        act = nc.scalar.activation
        V = nc.vector

        def load_w(src_ap, nelem):
            s = state["slot"]
            state["slot"] = (s + 1) % NSLOT
            dma(POOL, lambda: nc.gpsimd.dma_start(out=wring[s][:, 0:nelem], in_=src_ap), wsem[s],
                writes=[R_w[s]])
            return s

        def mm_group(bank, n, pairs, reads):
            def fn():
                ins = None
                last = len(pairs) - 1
                for i, (l_, r_) in enumerate(pairs):
                    ins = nc.tensor.matmul(PS(bank, 0, n), l_, r_, start=(i == 0), stop=(i == last))
                return ins
            return op(PE, fn, reads=reads, writes=[R_ps[bank]], pe=True)

        def ones_mm(bank, n, rhs_ap, rhs_res, first, last):
            return op(PE, lambda: nc.tensor.matmul(PS(bank, 0, n), ones[:, :], rhs_ap, start=first, stop=last),
                      reads=[rhs_res, R_ones], writes=[R_ps[bank]], pe=True)

        def ones_mm_bf(bank, n, rhs_ap, rhs_res, first, last):
            return op(PE, lambda: nc.tensor.matmul(PS(bank, 0, n), onesb[:, :], rhs_ap, start=first, stop=last),
                      reads=[rhs_res, R_ones], writes=[R_ps[bank]], pe=True)

        def proj(slot, rhs_fn, rhs_res, w0, t1):
            bank = alloc_bank()
            pairs = [(wring[slot][:, kt * 128:(kt + 1) * 128], rhs_fn(kt, w0, t1)) for kt in range(16)]
            mm_group(bank, t1 - w0, pairs, [R_w[slot]] + rhs_res)
            return bank

        def finish_rstd(banks, tl, r0, dim, epscol, chunk, use_mask):
            rs = nrs()
            for ti, (t0, t1) in enumerate(tl):
                n = t1 - t0
                op(ACT, lambda: act(out=rsb[rs][:, t0:t1], in_=PS(banks[ti], 0, n), func=AF.Sqrt,
                                    bias=epsb[:, epscol:epscol + 1], scale=1.0 / dim),
                   reads=[R_ps[banks[ti]], R_eps], writes=[R_rs[rs]])
            op(DVE, lambda: V.reciprocal(out=rsb[rs][:, r0:T], in_=rsb[rs][:, r0:T]),
               reads=[R_rs[rs]], writes=[R_rs[rs]])
            if use_mask and chunk == 0 and r0 < HALO:
                op(DVE, lambda: V.tensor_scalar(out=rsb[rs][:, r0:HALO], in0=rsb[rs][:, r0:HALO],
                                                scalar1=cvec[:, 0:1], scalar2=None, op0=ALU.mult),
                   reads=[R_rs[rs], R_cvec], writes=[R_rs[rs]])
            for b in banks:
                state["pinned"].discard(b)
            return rs

        def pre_norm(l, r0, goff):
            tl = tiles_of(r0)
            banks = [alloc_bank(), alloc_bank()]
            state["pinned"].update(banks)
            for ft in range(NF):
                sc = nscr()
                op(ACT, lambda: act(out=scrb[sc][:, r0:T], in_=X(ft, r0, T), func=AF.Square),
                   reads=[R_x[ft]], writes=[R_scr[sc]])
                for ti, (t0, t1) in enumerate(tl):
                    ones_mm_bf(banks[ti], t1 - t0, scrb[sc][:, t0:t1], R_scr[sc], ft == 0, ft == NF - 1)
            rs = finish_rstd(banks, tl, r0, float(D), 0, None, False)
            for ft in range(NF):
                op(DVE, lambda: V.scalar_tensor_tensor(out=H(ft, r0, T), in0=X(ft, r0, T),
                                                       scalar=pv(l, goff + ft), in1=rsb[rs][:, r0:T],
                                                       op0=ALU.mult, op1=ALU.mult),
                   reads=[R_x[ft], R_rs[rs], R_pv], writes=[R_o[ft // 2]])

        def post_norm_residual(l, r0, goff, banks, tl, chunk):
            rs = finish_rstd(banks, tl, r0, float(D), 0, chunk, True)
            for m in range(NF):
                op(DVE, lambda: V.scalar_tensor_tensor(out=O(m, r0, T), in0=O(m, r0, T),
                                                       scalar=pv(l, goff + m), in1=rsb[rs][:, r0:T],
                                                       op0=ALU.mult, op1=ALU.mult),
                   reads=[R_o[m], R_rs[rs], R_pv], writes=[R_o[m]])
                op(POOL, lambda: nc.gpsimd.tensor_tensor(out=X(m, r0, T), in0=X(m, r0, T), in1=O(m, r0, T),
                                                         op=ALU.add),
                   reads=[R_x[m], R_o[m]], writes=[R_x[m]])

        def evac_with_sq(bank, n, m, t0, t1, ssbank, first, last):
            op(DVE, lambda: V.tensor_copy(out=O(m, t0, t1), in_=PS(bank, 0, n)),
               reads=[R_ps[bank]], writes=[R_o[m]])

        def o_stats(r0, banks, tl):
            for m in range(NF):
                sc = nscr()
                op(ACT, lambda: act(out=scrb[sc][:, r0:T], in_=O(m, r0, T), func=AF.Square),
                   reads=[R_o[m]], writes=[R_scr[sc]])
                for ti, (t0, t1) in enumerate(tl):
                    ones_mm_bf(banks[ti], t1 - t0, scrb[sc][:, t0:t1], R_scr[sc], m == 0, m == NF - 1)

        def layer(l, chunk):
            first = (chunk == 0)
            last_chunk = (chunk == n_chunks - 1)
            if first:
                a = 32 * l
                b = 32 * l + 28
                hm0, hf0 = a, b - 2
            else:
                a = HALO - 30
                b = HALO
                hm0, hf0 = HALO, HALO
            tla = tiles_of(a)
            tlb = tiles_of(b)
            tlo = [(b, T)] if (T - b) <= 512 else tlb
            hres = [R_o[i] for i in range(8)]
            hT3 = hT[:, 0:NF * T].rearrange("p (f t) -> p f t", t=T)
            shm3 = save_hm[:, l * NF * 30:(l + 1) * NF * 30].rearrange("p (f c) -> p f c", c=30)
            shf3 = save_hf[:, l * NF * 2:(l + 1) * NF * 2].rearrange("p (f c) -> p f c", c=2)

            pre_norm(l, hm0, OFF_GMP)
            if not first:
                op(DVE, lambda: V.tensor_copy(out=hT3[:, :, HALO - 30:HALO], in_=shm3),
                   reads=[R_shm[l]], writes=hres)
            if not last_chunk:
                op(DVE, lambda: V.tensor_copy(out=shm3, in_=hT3[:, :, T - 30:T]),
                   reads=hres, writes=[R_shm[l]])

            def b_proj(j):
                sv = load_w(win_d[l, 18 + j], 2048)
                sg = load_w(win_d[l, 24 + j], 2048)
                gi = j % 2
                for (t0, t1) in tla:
                    n = t1 - t0
                    bv = proj(sv, H, hres, t0, t1)
                    bg = proj(sg, H, hres, t0, t1)
                    sc = nscr()
                    op(ACT, lambda: act(out=scr[sc][:, 0:n], in_=PS(bg, 0, n), func=AF.Sigmoid),
                       reads=[R_ps[bg]], writes=[R_scr[sc]])
                    op(DVE, lambda: V.tensor_tensor(out=GLU[gi](t0, t1), in0=scr[sc][:, 0:n], in1=PS(bv, 0, n),
                                                    op=ALU.mult),
                       reads=[R_scr[sc], R_ps[bv]], writes=[R_glu[gi]])

            def b_conv(j):
                gi = j % 2
                r2 = nrs()
                acc2 = rsb[r2]
                op(ACT, lambda: act(out=YB(j, b, T), in_=GLU[gi](b - 30, T - 30), func=AF.Identity,
                                    bias=pv(l, OFF_CBB + j), scale=pv(l, OFF_CB + j * 31)),
                   reads=[R_glu[gi], R_pv], writes=[R_yb[j]])
                op(ACT, lambda: act(out=acc2[:, b:T], in_=GLU[gi](b - 29, T - 29), func=AF.Identity,
                                    scale=pv(l, OFF_CB + j * 31 + 1)),
                   reads=[R_glu[gi], R_pv], writes=[R_rs[r2]])
                for k in range(2, 31):
                    if k % 2 == 0:
                        dst, dres = YB(j, b, T), R_yb[j]
                    else:
                        dst, dres = acc2[:, b:T], R_rs[r2]
                    op(DVE, lambda: V.scalar_tensor_tensor(out=dst, in0=GLU[gi](b - 30 + k, T - 30 + k),
                                                           scalar=pv(l, OFF_CB + j * 31 + k), in1=dst,
                                                           op0=ALU.mult, op1=ALU.add),
                       reads=[R_glu[gi], R_pv, dres], writes=[dres])
                op(DVE, lambda: V.tensor_tensor(out=YB(j, b, T), in0=YB(j, b, T), in1=acc2[:, b:T], op=ALU.add),
                   reads=[R_yb[j], R_rs[r2]], writes=[R_yb[j]])

            def a_head(j):
                sb_ = load_w(win_d[l, j], 2048)
                sc_ = load_w(win_d[l, 6 + j], 2048)
                sv_ = load_w(win_d[l, 12 + j], 2048)
                ci = j % 2
                for (t0, t1) in tla:
                    n = t1 - t0
                    bb = proj(sb_, H, hres, t0, t1)
                    bc = proj(sc_, H, hres, t0, t1)
                    bv = proj(sv_, H, hres, t0, t1)
                    s1 = nscr()
                    op(ACT, lambda: act(out=scr[s1][:, 0:n], in_=PS(bc, 0, n), func=AF.Copy),
                       reads=[R_ps[bc]], writes=[R_scr[s1]])
                    op(DVE, lambda: V.tensor_tensor(out=CV[ci](t0, t1), in0=scr[s1][:, 0:n], in1=PS(bv, 0, n),
                                                    op=ALU.mult),
                       reads=[R_scr[s1], R_ps[bv]], writes=[R_cv[ci]])
                    s2 = nscr()
                    op(ACT, lambda: act(out=scr[s2][:, 0:n], in_=CV[ci](t0 - 2, t1 - 2), func=AF.Identity,
                                        scale=pv(l, OFF_CA + j * 3)),
                       reads=[R_cv[ci], R_pv], writes=[R_scr[s2]])
                    for k in (1, 2):
                        op(DVE, lambda: V.scalar_tensor_tensor(out=scr[s2][:, 0:n], in0=CV[ci](t0 - 2 + k, t1 - 2 + k),
                                                               scalar=pv(l, OFF_CA + j * 3 + k), in1=scr[s2][:, 0:n],
                                                               op0=ALU.mult, op1=ALU.add),
                           reads=[R_cv[ci], R_pv, R_scr[s2]], writes=[R_scr[s2]])
                    op(DVE, lambda: V.tensor_tensor(out=Y(j, t0, t1), in0=scr[s2][:, 0:n], in1=PS(bb, 0, n),
                                                    op=ALU.mult),
                       reads=[R_scr[s2], R_ps[bb]], writes=[R_y[j]])


            for j in range(6):
                b_proj(j)
                a_head(j)
                b_conv(j)

            for g in range(4):
                su = load_w(win_d[l, 30 + g], 2048)
                w = POOL_W[g]
                for (t0, t1) in tla:
                    n = t1 - t0
                    bu = proj(su, H, hres, t0, t1)
                    op(ACT, lambda: act(out=UC(t0, t1), in_=PS(bu, 0, n), func=AF.Copy),
                       reads=[R_ps[bu]], writes=[R_uc])
                src, src_res = UC, R_uc
                for k in range(1, g + 2):
                    sh = 1 << (k - 1)
                    lo = a - (w - (1 << k))
                    di = (k - 1) % 2
                    dst, dst_res = SS[di], R_s[di]
                    op(DVE, lambda: V.tensor_tensor(out=dst(lo, T), in0=src(lo, T), in1=src(lo - sh, T - sh),
                                                    op=ALU.add),
                       reads=[src_res], writes=[dst_res])
                    src, src_res = dst, dst_res
                op(DVE, lambda: V.scalar_tensor_tensor(out=Y(12 + g, a, T), in0=src(a, T), scalar=1.0 / w,
                                                       in1=UC(a, T), op0=ALU.mult, op1=ALU.subtract),
                   reads=[src_res, R_uc], writes=[R_y[12 + g]])
                if chunk == 0:
                    sc = nscr()
                    op(DVE, lambda: V.tensor_tensor(out=scr[sc][:, 0:16], in0=src(HALO, HALO + 16),
                                                    in1=cvec[:, 1 + g * 16: 17 + g * 16], op=ALU.mult),
                       reads=[src_res, R_cvec], writes=[R_scr[sc]])
                    op(DVE, lambda: V.tensor_tensor(out=Y(12 + g, HALO, HALO + 16), in0=scr[sc][:, 0:16],
                                                    in1=UC(HALO, HALO + 16), op=ALU.subtract),
                       reads=[R_scr[sc], R_uc], writes=[R_y[12 + g]])
                for (t0, t1) in tla:
                    n = t1 - t0
                    bm = alloc_bank()
                    pw_ap = poolw[:, (l * 4 + g) * 128:(l * 4 + g + 1) * 128]
                    op(PE, lambda: nc.tensor.matmul(PS(bm, 0, n), pw_ap, Y(12 + g, t0, t1), start=True, stop=True),
                       reads=[R_pw, R_y[12 + g]], writes=[R_ps[bm]], pe=True)
                    op(DVE, lambda: V.tensor_scalar(out=Y(12 + g, t0, t1), in0=PS(bm, 0, n),
                                                    scalar1=pv(l, OFF_PS + g), scalar2=None, op0=ALU.mult),
                       reads=[R_ps[bm], R_pv], writes=[R_y[12 + g]])

            b1 = [alloc_bank(), alloc_bank()]
            b2 = [alloc_bank(), alloc_bank()]
            state["pinned"].update(b1 + b2)
            for j in range(6):
                for ti, (t0, t1) in enumerate(tlb):
                    ones_mm(b1[ti], t1 - t0, YB(j, t0, t1), R_yb[j], j == 0, j == 5)
            for j in range(6):
                sc = nscr()
                op(ACT, lambda: act(out=scrb[sc][:, b:T], in_=YB(j, b, T), func=AF.Square),
                   reads=[R_yb[j]], writes=[R_scr[sc]])
                for ti, (t0, t1) in enumerate(tlb):
                    ones_mm_bf(b2[ti], t1 - t0, scrb[sc][:, t0:t1], R_scr[sc], j == 0, j == 5)
            rmean, rvar, rnmr = nrs(), nrs(), nrs()
            for ti, (t0, t1) in enumerate(tlb):
                n = t1 - t0
                op(DVE, lambda: V.tensor_scalar(out=rsb[rmean][:, t0:t1], in0=PS(b1[ti], 0, n),
                                                scalar1=1.0 / 768.0, scalar2=None, op0=ALU.mult),
                   reads=[R_ps[b1[ti]]], writes=[R_rs[rmean]])
            sc = nscr()
            op(DVE, lambda: V.tensor_tensor(out=scr[sc][:, b:T], in0=rsb[rmean][:, b:T], in1=rsb[rmean][:, b:T],
                                            op=ALU.mult),
               reads=[R_rs[rmean]], writes=[R_scr[sc]])
            for ti, (t0, t1) in enumerate(tlb):
                n = t1 - t0
                op(DVE, lambda: V.scalar_tensor_tensor(out=rsb[rvar][:, t0:t1], in0=PS(b2[ti], 0, n),
                                                       scalar=1.0 / 768.0, in1=scr[sc][:, t0:t1],
                                                       op0=ALU.mult, op1=ALU.subtract),
                   reads=[R_ps[b2[ti]], R_scr[sc]], writes=[R_rs[rvar]])
            for bb in b1 + b2:
                state["pinned"].discard(bb)
            op(ACT, lambda: act(out=rsb[rvar][:, b:T], in_=rsb[rvar][:, b:T], func=AF.Sqrt,
                                bias=epsb[:, 1:2], scale=1.0),
               reads=[R_rs[rvar], R_eps], writes=[R_rs[rvar]])
            op(DVE, lambda: V.reciprocal(out=rsb[rvar][:, b:T], in_=rsb[rvar][:, b:T]),
               reads=[R_rs[rvar]], writes=[R_rs[rvar]])
            op(DVE, lambda: V.scalar_tensor_tensor(out=rsb[rnmr][:, b:T], in0=rsb[rmean][:, b:T], scalar=-1.0,
                                                   in1=rsb[rvar][:, b:T], op0=ALU.mult, op1=ALU.mult),
               reads=[R_rs[rmean], R_rs[rvar]], writes=[R_rs[rnmr]])
            for j in range(6):
                sc = nscr()
                op(DVE, lambda: V.tensor_tensor(out=scr[sc][:, b:T], in0=YB(j, b, T), in1=rsb[rvar][:, b:T],
                                                op=ALU.mult),
                   reads=[R_yb[j], R_rs[rvar]], writes=[R_scr[sc]])
                op(DVE, lambda: V.tensor_tensor(out=scr[sc][:, b:T], in0=scr[sc][:, b:T], in1=rsb[rnmr][:, b:T],
                                                op=ALU.add),
                   reads=[R_scr[sc], R_rs[rnmr]], writes=[R_scr[sc]])
                op(ACT, lambda: act(out=Y(6 + j, b, T), in_=scr[sc][:, b:T], func=AF.Silu,
                                    bias=pv(l, OFF_LNB + j), scale=pv(l, OFF_LNG + j)),
                   reads=[R_scr[sc], R_pv], writes=[R_y[6 + j]])

            bss = [alloc_bank() for _ in tlo]
            state["pinned"].update(bss)
            for m in range(NF):
                s = load_w(wout_d[l, m], 2048)
                for ti, (t0, t1) in enumerate(tlo):
                    bank = proj(s, Y, R_y, t0, t1)
                    evac_with_sq(bank, t1 - t0, m, t0, t1, bss[ti], m == 0, m == NF - 1)
            o_stats(b, bss, tlo)
            post_norm_residual(l, b, OFF_GMO, bss, tlo, chunk)

            pre_norm(l, hf0, OFF_GFP)
            if not first:
                op(DVE, lambda: V.tensor_copy(out=hT3[:, :, HALO - 2:HALO], in_=shf3),
                   reads=[R_shf[l]], writes=hres)
            if not last_chunk:
                op(DVE, lambda: V.tensor_copy(out=shf3, in_=hT3[:, :, T - 2:T]),
                   reads=hres, writes=[R_shf[l]])
            for j in range(NFF):
                sg = load_w(wup_d[l, j], 2048)
                sv = load_w(wup_d[l, NFF + j], 2048)
                for (t0, t1) in tlb:
                    n = t1 - t0
                    w0 = t0 - 2
                    bg = proj(sg, H, hres, w0, t1)
                    bv = proj(sv, H, hres, w0, t1)
                    accs = []
                    for (bank, m) in ((bg, j), (bv, NFF + j)):
                        sc = nscr()
                        op(DVE, lambda: V.tensor_scalar(out=scr[sc][:, 0:n], in0=PS(bank, 2, n + 2),
                                                        scalar1=pv(l, OFF_CF + m * 3 + 2), scalar2=pv(l, OFF_CFB + m),
                                                        op0=ALU.mult, op1=ALU.add),
                           reads=[R_ps[bank], R_pv], writes=[R_scr[sc]])
                        for k in (1, 0):
                            op(DVE, lambda: V.scalar_tensor_tensor(out=scr[sc][:, 0:n], in0=PS(bank, k, n + k),
                                                                   scalar=pv(l, OFF_CF + m * 3 + k),
                                                                   in1=scr[sc][:, 0:n], op0=ALU.mult, op1=ALU.add),
                               reads=[R_ps[bank], R_pv, R_scr[sc]], writes=[R_scr[sc]])
                        accs.append(sc)
                    sg_, sv_ = accs
                    op(ACT, lambda: act(out=scr[sg_][:, 0:n], in_=scr[sg_][:, 0:n], func=AF.Silu),
                       reads=[R_scr[sg_]], writes=[R_scr[sg_]])
                    op(DVE, lambda: V.tensor_tensor(out=A(j, t0, t1), in0=scr[sg_][:, 0:n], in1=scr[sv_][:, 0:n],
                                                    op=ALU.mult),
                       reads=[R_scr[sg_], R_scr[sv_]], writes=[R_act[j]])

            bss = [alloc_bank() for _ in tlo]
            state["pinned"].update(bss)
            for m in range(NF):
                s0 = load_w(wdn_d[l, 2 * m], WSLOT)
                s1 = load_w(wdn_d[l, 2 * m + 1], WSLOT)
                for ti, (t0, t1) in enumerate(tlo):
                    bank = alloc_bank()
                    pairs = []
                    for kt in range(NFF):
                        ws = wring[s0] if kt < 22 else wring[s1]
                        kk = kt % 22
                        pairs.append((ws[:, kk * 128:(kk + 1) * 128], A(kt, t0, t1)))
                    mm_group(bank, t1 - t0, pairs, [R_w[s0], R_w[s1]] + R_act)
                    evac_with_sq(bank, t1 - t0, m, t0, t1, bss[ti], m == 0, m == NF - 1)
            o_stats(b, bss, tlo)
            post_norm_residual(l, b, OFF_GFO, bss, tlo, chunk)

        dma(SP, lambda: nc.sync.dma_start(out=pvec[:, :], in_=pvec_d), csem[0], writes=[R_pv])
        dma(SP, lambda: nc.sync.dma_start(out=cvec[:, :], in_=cvec_d), csem[1], writes=[R_cvec])
        dma(POOL, lambda: nc.gpsimd.dma_start(out=poolw[:, :], in_=pool_d), csem[2], writes=[R_pw])
        op(DVE, lambda: V.memset(ones[:, :], 1.0), writes=[R_ones])
        op(DVE, lambda: V.memset(onesb[:, :], 1.0), writes=[R_ones])
        op(DVE, lambda: V.memset(epsb[:, 0:1], 1e-6), writes=[R_eps])
        op(DVE, lambda: V.memset(epsb[:, 1:2], 1e-5), writes=[R_eps])
        op(DVE, lambda: V.memset(oT[:, :], 0.0), writes=R_o)
        op(DVE, lambda: V.memset(bigf[:, :], 0.0),
           writes=R_y + R_yb + R_cv + R_glu + [R_uc] + R_s + R_act)
        for i in range(NSCR):
            op(DVE, lambda: V.memset(scr[i][:, :], 0.0), writes=[R_scr[i]])
        for i in range(NRS):
            op(DVE, lambda: V.memset(rsb[i][:, :], 0.0), writes=[R_rs[i]])

        xT3 = xT[:, :].rearrange("p (f t) -> p f t", t=T)
        for c in range(n_chunks):
            dma(SP, lambda: nc.sync.dma_start(out=xT3, in_=xT_d[:, :, c * CH: c * CH + T]), xsem, writes=R_x)
            for l in range(n_layers):
                layer(l, c)
            dma(SP, lambda: nc.sync.dma_start(out=yT_d[:, :, c * CH:(c + 1) * CH], in_=xT3[:, :, HALO:T]),
                ysem, reads=R_x)
        nc.sync.wait_ge(ysem.h, ysem.cnt)
    return nc


def _fm(vec):
    return np.ascontiguousarray(vec.reshape(-1, 128).T)


def _pack_layer_params(p, l):
    cols = [
        _fm(p["norm_mix_pre"][l]), _fm(p["norm_mix_post"][l]),
        _fm(p["norm_ffn_pre"][l]), _fm(p["norm_ffn_post"][l]),
        p["conv_a_w"][l].reshape(3, 6, 128).transpose(2, 1, 0).reshape(128, 18),
        p["conv_b_w"][l].reshape(31, 6, 128).transpose(2, 1, 0).reshape(128, 186),
        _fm(p["conv_b_bias"][l]), _fm(p["ln_b_gain"][l]), _fm(p["ln_b_bias"][l]),
        _fm(p["pool_scale"][l]),
        p["conv_ffn_w"][l].reshape(3, 88, 128).transpose(2, 1, 0).reshape(128, 264),
        _fm(p["conv_ffn_bias"][l]),
    ]
    out = np.concatenate(cols, axis=1).astype(np.float32)
    assert out.shape == (128, NPL)
    return out


def _tile_w(w, n_m):
    return np.ascontiguousarray(w.reshape(16, 128, n_m, 128).transpose(2, 1, 0, 3).reshape(n_m, 128, 2048))


def _tile_wdn(w):
    return np.ascontiguousarray(
        w.reshape(2, 22, 128, 16, 128).transpose(3, 0, 2, 1, 4).reshape(32, 128, WSLOT))


def _layer_weights(p, layers):
    win = np.stack([_tile_w(p["w_in"][l], 34) for l in layers])
    wout = np.stack([_tile_w(p["w_out"][l], 16) for l in layers])
    wup = np.stack([_tile_w(p["w_up"][l], 88) for l in layers])
    wdn = np.stack([_tile_wdn(p["w_down"][l]) for l in layers])
    poolw = np.concatenate([p["pool_w"][l].transpose(1, 0, 2).reshape(128, 4 * 128) for l in layers], axis=1)
    pvec = np.concatenate([_pack_layer_params(p, l) for l in layers], axis=1)
    return dict(win=win, wout=wout, wup=wup, wdn=wdn,
                poolw=np.ascontiguousarray(poolw.astype(np.float32)),
                pvec=np.ascontiguousarray(pvec))


def _core_consts(core):
    cv = np.zeros((128, 65), np.float32)
    start = (core * TOK_PER_CORE) % SEQ == 0
    cv[:, 0] = 0.0 if start else 1.0
    for g, w in enumerate(POOL_W):
        for i in range(16):
            cv[:, 1 + g * 16 + i] = 1.0 / (min(i + 1, w) if start else w)
    return cv


def _shard_x(xflat, n_chunks_total=4):
    outs = []
    for core in range(N_CORES):
        s = core * TOK_PER_CORE
        blk = np.zeros((HALO + TOK_PER_CORE, D), np.float32)
        if s % SEQ == 0:
            blk[HALO:] = xflat[s:s + TOK_PER_CORE]
        else:
            blk[:] = xflat[s - HALO:s + TOK_PER_CORE]
        outs.append(np.ascontiguousarray(blk.T.reshape(NF, 128, -1).transpose(1, 0, 2)))
    return outs


def _unshard_y(res):
    outs = []
    for r in res:
        yT = np.asarray(r["yT"])
        outs.append(yT.transpose(2, 1, 0).reshape(TOK_PER_CORE, D))
    return np.concatenate(outs, axis=0)


_NC_CACHE = {}


def _get_nc(n_layers, n_chunks):
    key = (n_layers, n_chunks)
    if key not in _NC_CACHE:
        _NC_CACHE[key] = build(n_layers, n_chunks)
    return _NC_CACHE[key]


FUSED = True


def kernel(**inputs):
    p = {k: np.asarray(v, dtype=np.float32) for k, v in inputs.items()}
    x = p["x"]
    B, S, _ = x.shape
    xflat = x.reshape(B * S, D)
    consts = [_core_consts(c) for c in range(N_CORES)]
    if FUSED:
        nc = _get_nc(DEPTH, 4)
        w = _layer_weights(p, range(DEPTH))
        xs = _shard_x(xflat)
        in_maps = [dict(xT=xs[c], cvec=consts[c], **w) for c in range(N_CORES)]
        res = run_bass_kernel_spmd(nc, in_maps, core_ids=list(range(N_CORES)))
        xflat = _unshard_y(res.results)
    else:
        nc = _get_nc(1, 4)
        for l in range(DEPTH):
            w = _layer_weights(p, [l])
            xs = _shard_x(xflat)
            in_maps = [dict(xT=xs[c], cvec=consts[c], **w) for c in range(N_CORES)]
            res = run_bass_kernel_spmd(nc, in_maps, core_ids=list(range(N_CORES)))
            xflat = _unshard_y(res.results)
    return xflat.reshape(B, S, D).astype(np.float32)
```
